# Optimizing a Trainium2 kernel written in Bass

```python
import math
import jax, jax.numpy as jnp
from jax import lax
import numpy as np

D_MODEL = 2048
BATCH = 16
SEQ = 2048
DEPTH = 2

N_META = 16
GRID_W = 64
ROPE_THETA = 10000.0
EPS = 1e-6

RET_HEADS = 8
RET_DK = 64
RET_DV = 128
RET_CHUNK = 128

DIFF_HEADS = 4
DIFF_DH = 128
DIFF_DV = 2 * DIFF_DH
Q_BLOCK = 128

NA_HEADS = 16
NA_DH = 64
NA_WIN_R = 8
NA_WIN_C = 16

RET_OUT_W = RET_HEADS * RET_DV
DIFF_OUT_W = DIFF_HEADS * DIFF_DV
NA_OUT_W = NA_HEADS * NA_DH

D_FF = 5504
CONV_W = 3

IN_SIZES = (
    RET_HEADS * RET_DK, RET_HEADS * RET_DK, RET_HEADS * RET_DV, RET_HEADS * RET_DV,
    2 * DIFF_HEADS * DIFF_DH, 2 * DIFF_HEADS * DIFF_DH, DIFF_HEADS * DIFF_DV,
    NA_HEADS * NA_DH, NA_HEADS * NA_DH, NA_HEADS * NA_DH,
    3 * D_MODEL,
)
IN_WIDTH = sum(IN_SIZES)

kernel_name = "hybrid_retention_diffattn_natten_encoder"

f32 = jnp.float32


def rms_norm(x, g):
    xf = x.astype(f32)
    y = xf * lax.rsqrt(jnp.mean(xf * xf, axis=-1, keepdims=True) + EPS)
    return (y * g.astype(f32)).astype(x.dtype)


def rope(x, pos):
    d = x.shape[-1]
    inv = jnp.power(ROPE_THETA, -jnp.arange(0, d, 2, dtype=f32) / d)
    ang = pos.astype(f32)[:, None] * inv[None, :]
    cos, sin = jnp.cos(ang), jnp.sin(ang)
    xf = x.astype(f32)
    x1, x2 = xf[..., : d // 2], xf[..., d // 2:]
    return jnp.concatenate([x1 * cos - x2 * sin, x1 * sin + x2 * cos], axis=-1).astype(x.dtype)


def to_heads(t, n):
    b, l, w = t.shape
    return t.reshape(b, l, n, w // n).transpose(0, 2, 1, 3)


def merge_heads(t):
    b, h, l, d = t.shape
    return t.transpose(0, 2, 1, 3).reshape(b, l, h * d)


def retention_scan(q, k, v, log_gamma, strict):
    b, h, lp, dk = q.shape
    dv = v.shape[-1]
    c = RET_CHUNK
    nc = lp // c
    qc = q.reshape(b, h, nc, c, dk)
    kc = k.reshape(b, h, nc, c, dk)
    vc = v.reshape(b, h, nc, c, dv)
    idx = jnp.arange(c, dtype=f32)
    diff = idx[:, None] - idx[None, :]
    keep = (diff > 0) if strict else (diff >= 0)
    lg = log_gamma[:, None, None]
    dmat = jnp.where(keep[None], jnp.exp(jnp.where(keep, diff, 0.0)[None] * lg), 0.0)
    s = jnp.einsum('bhncd,bhnsd->bhncs', qc, kc) * dmat[None, :, None].astype(q.dtype)
    o_intra = jnp.einsum('bhncs,bhnse->bhnce', s, vc)
    k_dec = jnp.exp((c - 1 - idx)[None, :] * log_gamma[:, None]).astype(q.dtype)
    kv = jnp.einsum('bhncd,bhnce->nbhde', kc * k_dec[None, :, None, :, None], vc).astype(f32)
    chunk_dec = jnp.exp(c * log_gamma)[None, :, None, None]

    def step(state, kv_n):
        return state * chunk_dec + kv_n, state

    _, prev = lax.scan(step, jnp.zeros((b, h, dk, dv), f32), kv)
    q_dec = jnp.exp((idx + 1)[None, :] * log_gamma[:, None]).astype(q.dtype)
    o_inter = jnp.einsum('bhncd,nbhde->bhnce', qc * q_dec[None, :, None, :, None], prev.astype(q.dtype))
    return (o_intra + o_inter).reshape(b, h, lp, dv)


def retention_branch(q, k, v, g, pos, log2_dec_f, log2_dec_b, out_gain):
    q = rope(to_heads(q, RET_HEADS), pos)
    k = rope(to_heads(k, RET_HEADS), pos) * (RET_DK ** -0.5)
    v = to_heads(v, RET_HEADS)
    L = q.shape[2]
    P = (-L) % RET_CHUNK
    lg_f = jnp.log1p(-jnp.exp2(-log2_dec_f.astype(f32)))
    lg_b = jnp.log1p(-jnp.exp2(-log2_dec_b.astype(f32)))
    pad_front = lambda t: jnp.pad(t, ((0, 0), (0, 0), (P, 0), (0, 0)))
    pad_back = lambda t: jnp.pad(t, ((0, 0), (0, 0), (0, P), (0, 0)))
    flip = lambda t: jnp.flip(t, axis=2)
    o_f = retention_scan(pad_front(q), pad_front(k), pad_front(v), lg_f, False)[:, :, P:]
    o_b = flip(retention_scan(pad_back(flip(q)), pad_back(flip(k)), pad_back(flip(v)), lg_b, True)[:, :, :L])
    o = rms_norm(o_f + o_b, out_gain.reshape(RET_HEADS, 1, RET_DV))
    return merge_heads(o) * jax.nn.silu(g)


def diff_branch(q, k, v, pos, q_gain, k_gain, lam_vecs, out_gain, lambda_init):
    b, L, _ = q.shape
    q = q.reshape(b, L, DIFF_HEADS, 2, DIFF_DH).transpose(0, 2, 3, 1, 4)
    k = k.reshape(b, L, DIFF_HEADS, 2, DIFF_DH).transpose(0, 2, 3, 1, 4)
    v = to_heads(v, DIFF_HEADS)
    q = rope(rms_norm(q, q_gain), pos)
    k = rope(rms_norm(k, k_gain), pos)
    lv = lam_vecs.astype(f32)
    lam = jnp.exp(jnp.sum(lv[0] * lv[1])) - jnp.exp(jnp.sum(lv[2] * lv[3])) + lambda_init
    scale = DIFF_DH ** -0.5
    P = (-L) % Q_BLOCK
    nb = (L + P) // Q_BLOCK
    qp = jnp.pad(q, ((0, 0), (0, 0), (0, 0), (0, P), (0, 0)))
    qb = qp.reshape(b, DIFF_HEADS, 2, nb, Q_BLOCK, DIFF_DH).transpose(3, 0, 1, 2, 4, 5)

    def block(qblk):
        s = jnp.einsum('bhiqd,bhikd->bhiqk', qblk, k).astype(f32) * scale
        p = jax.nn.softmax(s, axis=-1)
        a = p[:, :, 0] - lam * p[:, :, 1]
        return jnp.einsum('bhqk,bhke->bhqe', a.astype(v.dtype), v)

    o = lax.map(block, qb)
    o = o.transpose(1, 2, 0, 3, 4).reshape(b, DIFF_HEADS, nb * Q_BLOCK, DIFF_DV)[:, :, :L]
    o = rms_norm(o, out_gain) * (1.0 - lambda_init)
    return merge_heads(o)


def na_branch(q, k, v, q_gain, k_gain, rpb):
    q = rms_norm(to_heads(q, NA_HEADS), q_gain)
    k = rms_norm(to_heads(k, NA_HEADS), k_gain)
    v = to_heads(v, NA_HEADS)
    b, h, L, d = q.shape
    scale = NA_DH ** -0.5
    qm, qr = q[:, :, :N_META], q[:, :, N_META:]
    km, kr = k[:, :, :N_META], k[:, :, N_META:]
    vm, vr = v[:, :, :N_META], v[:, :, N_META:]
    S = qr.shape[2]
    rows = S // GRID_W
    wr = min(NA_WIN_R, rows)
    pm = jax.nn.softmax(jnp.einsum('bhqd,bhkd->bhqk', qm, km).astype(f32) * scale, axis=-1)
    om = jnp.einsum('bhqk,bhkd->bhqd', pm.astype(v.dtype), vm)
    qg = qr.reshape(b, h, rows, GRID_W, d)
    kg = kr.reshape(b, h, rows, GRID_W, d)
    vg = vr.reshape(b, h, rows, GRID_W, d)
    col = jnp.arange(GRID_W)
    cstart = jnp.clip(col - NA_WIN_C // 2, 0, GRID_W - NA_WIN_C)
    col_ok = (col[None, :] >= cstart[:, None]) & (col[None, :] < cstart[:, None] + NA_WIN_C)
    dc_idx = jnp.clip(col[None, :] - col[:, None] + NA_WIN_C - 1, 0, 2 * NA_WIN_C - 2)
    rpb_c = rpb.astype(f32)[:, :, dc_idx]
    bias_c = jnp.where(col_ok[None, None], rpb_c, -1e30)

    def row_block(r):
        rs = jnp.clip(r - wr // 2, 0, rows - wr)
        q_r = lax.dynamic_index_in_dim(qg, r, axis=2, keepdims=False)
        k_b = lax.dynamic_slice_in_dim(kg, rs, wr, axis=2)
        v_b = lax.dynamic_slice_in_dim(vg, rs, wr, axis=2)
        dr_idx = rs + jnp.arange(wr) - r + NA_WIN_R - 1
        bias = jnp.take(bias_c, dr_idx, axis=1).transpose(0, 2, 1, 3)
        s_win = jnp.einsum('bhqd,bhrkd->bhqrk', q_r, k_b).astype(f32) * scale + bias[None]
        s_win = s_win.reshape(b, h, GRID_W, wr * GRID_W)
        s_meta = jnp.einsum('bhqd,bhmd->bhqm', q_r, km).astype(f32) * scale
        p = jax.nn.softmax(jnp.concatenate([s_win, s_meta], axis=-1), axis=-1).astype(v.dtype)
        nw = wr * GRID_W
        return (jnp.einsum('bhqk,bhkd->bhqd', p[..., :nw], v_b.reshape(b, h, nw, d))
                + jnp.einsum('bhqm,bhmd->bhqd', p[..., nw:], vm))

    o_rows = lax.map(row_block, jnp.arange(rows))
    o_real = o_rows.transpose(1, 2, 0, 3, 4).reshape(b, h, S, d)
    return merge_heads(jnp.concatenate([om, o_real], axis=2))


def conv_ffn(h, w_up, conv_w, conv_b, w_down):
    u = h @ w_up
    gate, val = u[..., :D_FF], u[..., D_FF:]
    gp = jnp.pad(gate, ((0, 0), (1, 1), (0, 0)))
    gate = gp[:, :-2] * conv_w[0] + gp[:, 1:-1] * conv_w[1] + gp[:, 2:] * conv_w[2] + conv_b
    return (jax.nn.silu(gate) * val) @ w_down


def setup_inputs(seed: int = 0) -> dict:
    key = jax.random.key(seed)
    ks = jax.random.split(key, 24)
    n = lambda k, shape, s: jax.random.normal(k, shape, f32) * s
    gain = lambda k, shape: 1.0 + 0.02 * jax.random.normal(k, shape, f32)
    base_dec = 5.0 + jnp.arange(RET_HEADS, dtype=f32)
    return {
        "x": n(ks[0], (BATCH, SEQ, D_MODEL), 1.0),
        "meta_tokens": n(ks[1], (N_META, D_MODEL), 1.0),
        "norm_mix": gain(ks[2], (DEPTH, D_MODEL)),
        "w_in": n(ks[3], (DEPTH, D_MODEL, IN_WIDTH), D_MODEL ** -0.5),
        "ret_log2_decay_f": base_dec[None] + 0.1 * jax.random.normal(ks[4], (DEPTH, RET_HEADS), f32),
        "ret_log2_decay_b": base_dec[None] + 0.1 * jax.random.normal(ks[5], (DEPTH, RET_HEADS), f32),
        "ret_out_gain": gain(ks[6], (DEPTH, RET_OUT_W)),
        "diff_q_gain": gain(ks[7], (DEPTH, DIFF_DH)),
        "diff_k_gain": gain(ks[8], (DEPTH, DIFF_DH)),
        "diff_lambda": n(ks[9], (DEPTH, 4, DIFF_DH), 0.1),
        "diff_out_gain": gain(ks[10], (DEPTH, DIFF_DV)),
        "na_q_gain": gain(ks[11], (DEPTH, NA_DH)),
        "na_k_gain": gain(ks[12], (DEPTH, NA_DH)),
        "na_rpb": n(ks[13], (DEPTH, NA_HEADS, 2 * NA_WIN_R - 1, 2 * NA_WIN_C - 1), 0.1),
        "w_br_ret": n(ks[14], (DEPTH, RET_OUT_W, D_MODEL), RET_OUT_W ** -0.5),
        "w_br_diff": n(ks[15], (DEPTH, DIFF_OUT_W, D_MODEL), DIFF_OUT_W ** -0.5),
        "w_br_na": n(ks[16], (DEPTH, NA_OUT_W, D_MODEL), NA_OUT_W ** -0.5),
        "w_out": n(ks[17], (DEPTH, D_MODEL, D_MODEL), D_MODEL ** -0.5),
        "norm_ffn": gain(ks[18], (DEPTH, D_MODEL)),
        "w_up": n(ks[19], (DEPTH, D_MODEL, 2 * D_FF), D_MODEL ** -0.5),
        "ffn_conv_w": n(ks[20], (DEPTH, CONV_W, D_FF), CONV_W ** -0.5),
        "ffn_conv_b": n(ks[21], (DEPTH, D_FF), 0.02),
        "w_down": n(ks[22], (DEPTH, D_FF, D_MODEL), D_FF ** -0.5),
    }


def reference(x, meta_tokens, norm_mix, w_in, ret_log2_decay_f, ret_log2_decay_b, ret_out_gain,
              diff_q_gain, diff_k_gain, diff_lambda, diff_out_gain, na_q_gain, na_k_gain, na_rpb,
              w_br_ret, w_br_diff, w_br_na, w_out, norm_ffn, w_up, ffn_conv_w, ffn_conv_b, w_down):
    b = x.shape[0]
    meta = jnp.broadcast_to(meta_tokens[None].astype(x.dtype), (b, N_META, x.shape[-1]))
    h = jnp.concatenate([meta, x], axis=1)
    L = h.shape[1]
    pos = jnp.arange(L)
    split_pts = np.cumsum(IN_SIZES)[:-1].tolist()
    for l in range(DEPTH):
        lambda_init = 0.8 - 0.6 * math.exp(-0.3 * l)
        u = rms_norm(h, norm_mix[l])
        (rq, rk, rv, rg, dq, dk, dv, nq, nk, nv, gates) = jnp.split(u @ w_in[l], split_pts, axis=-1)
        y_ret = retention_branch(rq, rk, rv, rg, pos, ret_log2_decay_f[l], ret_log2_decay_b[l], ret_out_gain[l])
        y_diff = diff_branch(dq, dk, dv, pos, diff_q_gain[l], diff_k_gain[l], diff_lambda[l],
                             diff_out_gain[l], lambda_init)
        y_na = na_branch(nq, nk, nv, na_q_gain[l], na_k_gain[l], na_rpb[l])
        g_ret, g_diff, g_na = jnp.split(jax.nn.sigmoid(gates), 3, axis=-1)
        merged = (g_ret * (y_ret @ w_br_ret[l]) + g_diff * (y_diff @ w_br_diff[l])
                  + g_na * (y_na @ w_br_na[l]))
        h = h + merged @ w_out[l]
        h = h + conv_ffn(rms_norm(h, norm_ffn[l]), w_up[l], ffn_conv_w[l], ffn_conv_b[l], w_down[l])
    return h[:, N_META:]
```

```python
import contextlib
import math
import numpy as np
import concourse.bass as bass
import concourse.mybir as mybir
from concourse.bass_utils import run_bass_kernel_spmd

F32 = mybir.dt.float32
BF16 = mybir.dt.bfloat16
AF = mybir.ActivationFunctionType
ALU = mybir.AluOpType
AX = mybir.AxisListType

D = 2048
SEQ = 2048
NMETA = 16
L = SEQ + NMETA
DEPTH = 2
INW = 15360
DFF = 5504
EPS = 1e-6
NCORES = 8
RET_CW = 6912
ALL_STAGES = ('proj', 'ret', 'diff', 'na', 'merge', 'ffn')

TILES = [(0, 16)] + [(16 + 128 * t, 128) for t in range(16)]
BLOCKS = [(0, 16)] + [(16 + 512 * b, 512) for b in range(4)]

C_RQ, C_RK, C_RV, C_RG = 0, 512, 1024, 2048
C_DQ, C_DK, C_DV = 3072, 4096, 5120
C_NQ, C_NK, C_NV = 6144, 7168, 8192
C_GATES = 9216


class Ctx:
    NDMA = 20

    def __init__(self, nc):
        self.nc = nc
        self.es = contextlib.ExitStack()
        self.eng = {"pe": nc.tensor, "act": nc.scalar, "dve": nc.vector, "pool": nc.gpsimd, "sp": nc.sync}
        self.sem = {}
        self.cnt = {}
        self.seen = {e: {} for e in self.eng}
        for e in self.eng:
            self.sem[e] = self.es.enter_context(nc.semaphore("s_" + e))
            self.cnt[e] = 0
        self.dslots = {}
        self.dnext = {}
        for q in ("sp", "pool", "act"):
            n = self.NDMA
            self.dslots[q] = [[self.es.enter_context(nc.semaphore("d_%s%d" % (q, i))), 0] for i in range(n)]
            self.dnext[q] = 0
        self.lastw = {}
        self.readers = {}
        self.nwaits = 0

    def _handle(self, key):
        if isinstance(key, str):
            return self.sem[key]
        return self.dslots[key[0]][key[1]][0]

    def _wait(self, e, key, val):
        if val <= 0:
            return
        if self.seen[e].get(key, 0) >= val:
            return
        self.eng[e].wait_ge(self._handle(key), val)
        self.seen[e][key] = val
        self.nwaits += 1

    def op(self, e, fn, reads=(), writes=(), dma=False):
        deps = {}

        def add(tok):
            if tok is None:
                return
            k, v = tok
            if e == "pe" and k == "pe":
                return
            if deps.get(k, 0) < v:
                deps[k] = v

        for k in reads:
            for kk, vv in self.lastw.get(k, {}).items():
                add((kk, vv))
        for k in writes:
            for kk, vv in self.lastw.get(k, {}).items():
                add((kk, vv))
            for kk, vv in self.readers.get(k, {}).items():
                add((kk, vv))
        for k, v in deps.items():
            self._wait(e, k, v)
        if dma:
            i = self.dnext[e]
            self.dnext[e] = (i + 1) % len(self.dslots[e])
            slot = self.dslots[e][i]
            key = (e, i)
            self._wait(e, key, slot[1])
            inst = fn()
            inst.then_inc(slot[0], 16)
            slot[1] += 16
            tok = (key, slot[1])
        else:
            inst = fn()
            inst.then_inc(self.sem[e], 1)
            self.cnt[e] += 1
            tok = (e, self.cnt[e])
        for k in reads:
            r = self.readers.setdefault(k, {})
            if r.get(tok[0], 0) < tok[1]:
                r[tok[0]] = tok[1]
        for k in writes:
            w = self.lastw.setdefault(k, {})
            if w.get(tok[0], 0) < tok[1]:
                w[tok[0]] = tok[1]
            self.readers[k] = {}
        return tok

    def dma(self, q, out, in_, reads=(), writes=(), **kw):
        return self.op(q, lambda: self.eng[q].dma_start(out=out, in_=in_, **kw), reads, writes, dma=True)

    def barrier(self):
        for e in self.eng:
            for e2 in self.eng:
                if e2 != e:
                    self._wait(e, e2, self.cnt[e2])
            for q, slots in self.dslots.items():
                if q == "pool":
                    continue
                for i, s in enumerate(slots):
                    self._wait(e, (q, i), s[1])
        self.lastw = {k: v for k, v in self.lastw.items() if isinstance(k, tuple) and k[0] in ("WB", "WU", "WD2")}
        self.readers = {}

    def finish(self):
        for q, slots in self.dslots.items():
            for i, s in enumerate(slots):
                self._wait("sp", (q, i), s[1])


class Prefetch:
    def __init__(self, loaders):
        self.loaders = loaders
        self.done = 0

    def ensure(self, i):
        while self.done <= i and self.done < len(self.loaders):
            self.loaders[self.done]()
            self.done += 1


class Scope:
    G = 0

    def __init__(self, cx):
        self.cx = cx
        self.es = contextlib.ExitStack()
        self.n = 0

    def __enter__(self):
        self.es.__enter__()
        return self

    def __exit__(self, *a):
        self.cx.barrier()
        return self.es.__exit__(*a)

    def sb(self, shape, dt, name=None):
        Scope.G += 1
        return self.es.enter_context(self.cx.nc.sbuf_tensor("%s_%d" % (name or "t", Scope.G), list(shape), dt))


def build_program(nseq=2, nlayers=DEPTH, debug=None, stages=ALL_STAGES):
    debug = debug or set()
    nc = bass.Bass("TRN2", target_bir_lowering=False)
    cx = Ctx(nc)

    def dram(name, shape, dt, kind=None):
        if kind is None:
            kind = "ExternalOutput" if name in debug else "Internal"
        return nc.dram_tensor(name, list(shape), dt, kind=kind).ap()

    xin = dram("xin", [nseq, L, D], F32, "ExternalInput")
    hbuf = dram("hbuf", [nseq, L, D], F32, "ExternalOutput")
    W = {}
    wshapes = {
        "w_in": [DEPTH, D, INW], "w_br_ret": [DEPTH, 1024, D], "w_br_diff": [DEPTH, 1024, D],
        "w_br_na": [DEPTH, 1024, D], "w_out": [DEPTH, D, D], "w_up": [DEPTH, D, 2 * DFF],
        "w_down": [DEPTH, DFF, D],
    }
    for k, s in wshapes.items():
        W[k] = dram(k, s, F32, "ExternalInput")
    small = {
        "norm_mix": [DEPTH, D], "norm_ffn": [DEPTH, D], "ret_log2_decay_f": [DEPTH, 8], "ret_log2_decay_b": [DEPTH, 8],
        "ret_out_gain": [DEPTH, 1024], "diff_q_gain": [DEPTH, 128], "diff_k_gain": [DEPTH, 128],
        "diff_lambda": [DEPTH, 4, 128], "diff_out_gain": [DEPTH, 256], "na_q_gain": [DEPTH, 64],
        "na_k_gain": [DEPTH, 64], "ffn_conv_w": [DEPTH, 3, DFF], "ffn_conv_b": [DEPTH, DFF],
    }
    P = {k: dram(k, s, F32, "ExternalInput") for k, s in small.items()}
    c_ident = dram("c_ident", [128, 128], F32, "ExternalInput")
    c_rope_r = dram("c_rope_r", [L, 2, 32], F32, "ExternalInput")
    c_rope_d = dram("c_rope_d", [L, 2, 64], F32, "ExternalInput")

    WB = {k: dram(k + "_bf", s, BF16) for k, s in wshapes.items()}
    QKT = dram("QKT", [nseq, 40, 128, L], BF16)
    VTM = dram("VTM", [nseq, L, 3072], BF16)
    GT = dram("GT", [nseq, 56, 128, L], BF16)
    YT = dram("YT", [nseq, 24, 128, L], BF16)
    c_ret = dram("c_ret", [128, RET_CW], F32, "ExternalInput")
    MT = dram("MT", [nseq, 16, 128, L], BF16)
    na_bt = dram("na_bt", [DEPTH, 16, 128, 960], F32, "ExternalInput")
    c_cm = dram("c_cm", [128, 64], F32, "ExternalInput")

    ps = [cx.es.enter_context(nc.psum_tensor("ps%d" % i, [128, 512], F32)) for i in range(6)]
    psb = [cx.es.enter_context(nc.psum_tensor("psb%d" % i, [128, 1024], BF16)) for i in range(2)]

    cast_q = []

    def emit_cast(l, fn):
        if l == 0:
            fn()
        else:
            cast_q.append(fn)

    def pump(n=12):
        for _ in range(n):
            if cast_q:
                cast_q.pop(0)()

    def cast_weights(l):
        for k, s in wshapes.items():
            if k in ("w_up", "w_down"):
                continue
            rows, cols = s[1], s[2]
            cw = 1920 if cols % 1920 == 0 else (1376 if cols % 1376 == 0 else 2048)
            assert cols % cw == 0
            rg = 256 if rows % 256 == 0 else 128
            for c0 in range(0, cols, cw):
                for r0 in range(0, rows, rg):
                    key = ("WB", k, l, c0 // cw) if k == "w_in" else ("WB", k, l)
                    emit_cast(l, (lambda k=k, l=l, r0=r0, rg=rg, c0=c0, cw=cw, key=key:
                                  cx.dma("pool", WB[k][l, r0:r0 + rg, c0:c0 + cw], W[k][l, r0:r0 + rg, c0:c0 + cw], writes=[key])))

    WD2 = dram("WD2", [DEPTH, 8, 128, 43, 256], BF16)
    WU = dram("WU", [DEPTH, 43, 128, 16, 256], BF16)

    def cast_wdown(l):
        for nb in range(8):
            for fg in range(0, 43, 8):
                f1 = min(fg + 8, 43)
                emit_cast(l, (lambda l=l, nb=nb, fg=fg, f1=f1:
                              cx.dma("pool", WD2[l, nb, :, fg:f1, :],
                                     W["w_down"][l, fg * 128:f1 * 128, nb * 256:(nb + 1) * 256].rearrange("(f p) j -> p f j", p=128),
                                     writes=[("WD2", l)])))

    def cast_wup(l):
        for kc in range(16):
            for half in range(2):
                for fg in range(0, 43, 8):
                    f1 = min(fg + 8, 43)
                    emit_cast(l, (lambda l=l, fg=fg, f1=f1, kc=kc, half=half:
                                  cx.dma("pool", WU[l, fg:f1, :, kc, half * 128:(half + 1) * 128].rearrange("f p j -> p f j"),
                                         W["w_up"][l, kc * 128:(kc + 1) * 128, half * DFF + fg * 128:half * DFF + f1 * 128].rearrange("p (f j) -> p f j", j=128),
                                         writes=[("WU", l)])))

    for l in range(nlayers):
        cast_weights(l)
        cast_wup(l)
        cast_wdown(l)

    cs = Scope(cx)
    cs.es.__enter__()
    ident_f = cs.sb([128, 128], F32, "ident_f")
    ident = cs.sb([128, 128], BF16, "ident")
    cx.dma("sp", ident_f[:], c_ident[:, :], writes=["ident_f"])
    cx.op("dve", lambda: nc.vector.tensor_copy(out=ident[:], in_=ident_f[:]), reads=["ident_f"], writes=["ident"])
    ones_f = cs.sb([128, 128], F32, "ones_f")
    cx.op("dve", lambda: nc.vector.memset(ones_f[:], 1.0), writes=["ones_f"])
    ones_b = cs.sb([128, 128], BF16, "ones_b")
    cx.op("dve", lambda: nc.vector.memset(ones_b[:], 1.0), writes=["ones_b"])
    ones_m = cs.sb([128, 128], BF16, "ones_m")
    cx.op("dve", lambda: nc.vector.memset(ones_m[:], 0.0), writes=["ones_m"])
    cx.op("dve", lambda: nc.vector.memset(ones_m[0:16, :], 1.0), writes=["ones_m"])

    rr = [0]

    def psum_rot(n=6):
        i = rr[0] % n
        rr[0] += 1
        return i

    def norm_to_uT(sc, src, s, gain_ap, uT):
        g_bc = sc.sb([128, D], F32)
        cx.dma("sp", g_bc[:], gain_ap.partition_broadcast(128), writes=[("g_bc", id(g_bc))])
        hts = [sc.sb([128, D], F32) for _ in range(2)]
        sq = sc.sb([128, D], BF16)
        ubs = [sc.sb([128, D], BF16) for _ in range(2)]
        st = [sc.sb([128, 4], F32) for _ in range(2)]
        for ti, (p0, n) in enumerate(TILES):
            b = ti % 2
            ht, ub, stt = hts[b], ubs[b], st[b]
            cx.dma("sp", ht[:n, :], src[s, p0:p0 + n, :], writes=[("ht", b)])
            cx.op("act", lambda: nc.scalar.activation(out=sq[:n, :], in_=ht[:n, :], func=AF.Square,
                                                      accum_out=stt[:n, 0:1]),
                  reads=[("ht", b)], writes=["sq", ("st", b)])
            cx.op("act", lambda: nc.scalar.activation(out=stt[:n, 1:2], in_=stt[:n, 0:1], func=AF.Sqrt,
                                                      scale=1.0 / D, bias=eps_t[:n, 0:1]),
                  reads=[("st", b)], writes=[("st", b)])
            cx.op("dve", lambda: nc.vector.reciprocal(out=stt[:n, 2:3], in_=stt[:n, 1:2]),
                  reads=[("st", b)], writes=[("st", b)])
            cx.op("dve", lambda: nc.vector.scalar_tensor_tensor(out=ub[:n, :], in0=ht[:n, :], scalar=stt[:n, 2:3],
                                                                op0=ALU.mult, in1=g_bc[:n, :], op1=ALU.mult),
                  reads=[("ht", b), ("st", b), ("g_bc", id(g_bc))], writes=[("ub", b)])
            for g4 in range(4):
                pb = psum_rot(2)
                for j in range(4):
                    c = g4 * 4 + j
                    cx.op("pe", lambda: nc.tensor.transpose(out=psb[pb][:, j * 128:j * 128 + n],
                                                            in_=ub[:n, c * 128:(c + 1) * 128], identity=ident[:n, :n]),
                          reads=[("ub", b), "ident"], writes=[("psb", pb)])
                src_v = psb[pb][:, 0:512].rearrange("p (j t) -> p j t", j=4)[:, :, 0:n]
                cx.op("act" if g4 % 2 else "dve",
                      (lambda: nc.scalar.copy(out=uT[:, g4 * 4:g4 * 4 + 4, p0:p0 + n], in_=src_v)) if g4 % 2 else
                      (lambda: nc.vector.tensor_copy(out=uT[:, g4 * 4:g4 * 4 + 4, p0:p0 + n], in_=src_v)),
                      reads=[("psb", pb)], writes=[("uT", ti)])

    UT_KEYS = [("uT", ti) for ti in range(len(TILES))]

    def load_w_block(wt, wsrc, c0, ncols, key, kchunks=16):
        cx.dma("sp", wt[:, :kchunks, :ncols],
               wsrc[:, c0:c0 + ncols].rearrange("(k p) c -> p k c", p=128),
               reads=[key[0]], writes=[key[1]])

    def stage_proj(l, s, src):
        pe2 = "dve"
        E2 = nc.vector if pe2 == "dve" else nc.gpsimd
        with Scope(cx) as sc:
            uT = sc.sb([128, 16, L], BF16, "uT")
            wts = [sc.sb([128, 16, 512], BF16) for _ in range(2)]
            wsrc = WB["w_in"][l]
            all_c0 = ([C_RQ, C_RK, C_DQ, C_DQ + 512, C_DK, C_DK + 512, C_NQ, C_NQ + 512, C_NK, C_NK + 512] +
                      [C_RV, C_RV + 512, C_DV, C_DV + 512, C_NV, C_NV + 512] +
                      [C_RG, C_RG + 512] + [C_GATES + 512 * i for i in range(12)])
            def load_win(i, c0):
                rk = sorted(set([("WB", "w_in", l, c0 // 1920), ("WB", "w_in", l, (c0 + 511) // 1920)]))
                cx.dma("sp", wts[i % 2][:, :, :], wsrc[:, c0:c0 + 512].rearrange("(k p) c -> p k c", p=128),
                       reads=rk, writes=[("wt", i % 2)])

            pf = Prefetch([(lambda i=i, c0=c0: load_win(i, c0)) for i, c0 in enumerate(all_c0)])
            pf.ensure(0)
            with Scope(cx) as scn:
                norm_to_uT(scn, src, s, P["norm_mix"][l], uT)
            wi = [0]

            def next_w(c0):
                i = wi[0]
                wi[0] += 1
                assert all_c0[i] == c0
                pf.ensure(i + 1)
                return wts[i % 2], ("wt", i % 2)

            gq_d = sc.sb([128, 128], F32); gk_d = sc.sb([128, 128], F32)
            gq_n = sc.sb([128, 64], F32); gk_n = sc.sb([128, 64], F32)
            cx.dma("sp", gq_d[:], P["diff_q_gain"][l].partition_broadcast(128), writes=["gq_d"])
            cx.dma("sp", gk_d[:], P["diff_k_gain"][l].partition_broadcast(128), writes=["gk_d"])
            cx.dma("sp", gq_n[:], P["na_q_gain"][l].partition_broadcast(128), writes=["gq_n"])
            cx.dma("sp", gk_n[:], P["na_k_gain"][l].partition_broadcast(128), writes=["gk_n"])
            rope_r = sc.sb([128, 17, 64], F32); rope_d = sc.sb([128, 17, 128], F32)
            for ti, (p0, n) in enumerate(TILES):
                cx.dma("sp", rope_r[:n, ti, :], c_rope_r[p0:p0 + n].rearrange("p a b -> p (a b)"), writes=["rope_r"])
                cx.dma("sp", rope_d[:n, ti, :], c_rope_d[p0:p0 + n].rearrange("p a b -> p (a b)"), writes=["rope_d"])

            qk_blocks = [("rq", C_RQ), ("rk", C_RK), ("dq", C_DQ), ("dq", C_DQ + 512), ("dk", C_DK), ("dk", C_DK + 512),
                         ("nq", C_NQ), ("nq", C_NQ + 512), ("nk", C_NK), ("nk", C_NK + 512)]
            NE = 4
            xs_t = [sc.sb([128, 512], F32) for _ in range(NE)]
            sq_ts = [sc.sb([128, 512], F32) for _ in range(NE)]
            ss_t = [sc.sb([128, 24], F32) for _ in range(NE)]
            t_as = [[sc.sb([128, 256], F32) for _ in range(4)] for _ in range(NE)]
            xo_t = [sc.sb([128, 512], BF16) for _ in range(NE)]
            TT = [sc.sb([128, 4, L], BF16) for _ in range(2)]
            ei = [0]
            tq = []
            for bi, (kind, c0) in enumerate(qk_blocks):
                wt, wkey = next_w(c0)
                Tb = bi % 2
                T = TT[Tb]
                for ti, (p0, n) in enumerate(TILES):
                    pi = psum_rot()
                    for kc in range(16):
                        cx.op("pe", lambda: nc.tensor.matmul(ps[pi][:n, :], lhsT=uT[:, kc, p0:p0 + n], rhs=wt[:, kc, :],
                                                             start=(kc == 0), stop=(kc == 15)),
                              reads=[("uT", ti), wkey], writes=[("ps", pi)])
                    e = ei[0] % NE
                    ei[0] += 1
                    xs, ss, xo = xs_t[e], ss_t[e], xo_t[e]
                    sq_t = sq_ts[e]
                    t_a = t_as[e]
                    kx, ks, ko = ("xs", e), ("ss", e), ("xo", e)
                    ksq = ("sq_t", e)
                    kta = [("ta", e, i) for i in range(4)]
                    sc_k = 0.125 if kind == "rk" else 1.0
                    cx.op("act", lambda: nc.scalar.activation(out=xs[:n, :], in_=ps[pi][:n, :], func=AF.Copy, scale=sc_k),
                          reads=[("ps", pi)], writes=[kx])
                    if kind in ("dq", "dk", "nq", "nk"):
                        dd = 128 if kind[0] == "d" else 64
                        U = 512 // dd
                        gb = {"dq": gq_d, "dk": gk_d, "nq": gq_n, "nk": gk_n}[kind]
                        gkey = {"dq": "gq_d", "dk": "gk_d", "nq": "gq_n", "nk": "gk_n"}[kind]
                        cx.op("act", lambda: nc.scalar.activation(out=sq_t[:n, :], in_=ps[pi][:n, :], func=AF.Square),
                              reads=[("ps", pi)], writes=[ksq])
                        cx.op("dve", lambda: nc.vector.tensor_reduce(out=ss[:n, 0:U], in_=sq_t[:n, :].rearrange("p (u d) -> p u d", u=U),
                                                                     op=ALU.add, axis=AX.X),
                              reads=[ksq], writes=[ks])
                        cx.op("act", lambda: nc.scalar.activation(out=ss[:n, 8:8 + U], in_=ss[:n, 0:U], func=AF.Sqrt,
                                                                  scale=1.0 / dd, bias=eps_t[:n, 0:1]),
                              reads=[ks], writes=[ks])
                        cx.op("dve", lambda: nc.vector.reciprocal(out=ss[:n, 16:16 + U], in_=ss[:n, 8:8 + U]),
                              reads=[ks], writes=[ks])
                        xv = xs[:n, :].rearrange("p (u d) -> p u d", u=U)
                        cx.op("dve", lambda: nc.vector.tensor_tensor(out=xv, in0=xv, in1=ss[:n, 16:16 + U].unsqueeze(2).to_broadcast([n, U, dd]),
                                                                     op=ALU.mult),
                              reads=[kx, ks], writes=[kx])
                        dst = xv if kind[0] == "d" else xo[:n, :].rearrange("p (u d) -> p u d", u=U)
                        cx.op(pe2 if kind[0] == "d" else "dve",
                              (lambda: E2.tensor_tensor(out=dst, in0=xv, in1=gb[:n, :].unsqueeze(1).to_broadcast([n, U, dd]), op=ALU.mult))
                              if kind[0] == "d" else
                              (lambda: nc.vector.tensor_tensor(out=dst, in0=xv, in1=gb[:n, :].unsqueeze(1).to_broadcast([n, U, dd]), op=ALU.mult)),
                              reads=[kx, gkey], writes=[kx] if kind[0] == "d" else [ko])
                    if kind in ("rq", "rk", "dq", "dk"):
                        dd = 64 if kind[0] == "r" else 128
                        hd = dd // 2
                        U = 512 // dd
                        rt = rope_r if kind[0] == "r" else rope_d
                        rkey = "rope_r" if kind[0] == "r" else "rope_d"
                        xv = xs[:n, :].rearrange("p (u a d) -> p u a d", u=U, a=2)
                        ov = xo[:n, :].rearrange("p (u a d) -> p u a d", u=U, a=2)
                        x1, x2 = xv[:, :, 0, :], xv[:, :, 1, :]
                        cosb = rt[:n, ti, 0:hd].unsqueeze(1).to_broadcast([n, U, hd])
                        sinb = rt[:n, ti, hd:2 * hd].unsqueeze(1).to_broadcast([n, U, hd])
                        tv = [t[:n, :].rearrange("p (u d) -> p u d", u=U) for t in t_a]
                        cx.op("dve", lambda: nc.vector.tensor_tensor(out=tv[0], in0=x1, in1=cosb, op=ALU.mult), reads=[kx, rkey], writes=[kta[0]])
                        cx.op(pe2, lambda: E2.tensor_tensor(out=tv[1], in0=x2, in1=sinb, op=ALU.mult), reads=[kx, rkey], writes=[kta[1]])
                        cx.op("dve", lambda: nc.vector.tensor_tensor(out=tv[2], in0=x1, in1=sinb, op=ALU.mult), reads=[kx, rkey], writes=[kta[2]])
                        cx.op(pe2, lambda: E2.tensor_tensor(out=tv[3], in0=x2, in1=cosb, op=ALU.mult), reads=[kx, rkey], writes=[kta[3]])
                        cx.op("dve", lambda: nc.vector.tensor_tensor(out=ov[:, :, 0, :], in0=tv[0], in1=tv[1], op=ALU.subtract),
                              reads=[kta[0], kta[1]], writes=[ko])
                        cx.op(pe2, lambda: E2.tensor_tensor(out=ov[:, :, 1, :], in0=tv[2], in1=tv[3], op=ALU.add),
                              reads=[kta[2], kta[3]], writes=[ko])
                    def do_transposes(xo=xo, ko=ko, n=n, p0=p0, T=T, Tb=Tb):
                        pb = psum_rot(2)
                        for j in range(4):
                            cx.op("pe", lambda: nc.tensor.transpose(out=psb[pb][:, j * 128:j * 128 + n], in_=xo[:n, j * 128:(j + 1) * 128],
                                                                    identity=ident[:n, :n]),
                                  reads=[ko, "ident"], writes=[("psb", pb)])
                        src_v = psb[pb][:, 0:512].rearrange("p (j t) -> p j t", j=4)[:, :, 0:n]
                        cx.op("dve", lambda: nc.vector.tensor_copy(out=T[:, :, p0:p0 + n], in_=src_v),
                              reads=[("psb", pb)], writes=[("T", Tb)])
                    tq.append(do_transposes)
                    if len(tq) > 2:
                        tq.pop(0)()
                while tq:
                    tq.pop(0)()
                cx.dma("sp", QKT[s, bi * 4:(bi + 1) * 4].rearrange("j p t -> p j t"), T[:, :, :],
                       reads=[("T", Tb)], writes=[("QKT", s, bi)])

            v_blocks = [C_RV, C_RV + 512, C_DV, C_DV + 512, C_NV, C_NV + 512]
            vo_t = [sc.sb([128, 512], BF16) for _ in range(2)]
            for bi, c0 in enumerate(v_blocks):
                wt, wkey = next_w(c0)
                for ti, (p0, n) in enumerate(TILES):
                    pi = psum_rot()
                    for kc in range(16):
                        cx.op("pe", lambda: nc.tensor.matmul(ps[pi][:n, :], lhsT=uT[:, kc, p0:p0 + n], rhs=wt[:, kc, :],
                                                             start=(kc == 0), stop=(kc == 15)),
                              reads=[("uT", ti), wkey], writes=[("ps", pi)])
                    e = ei[0] % 2
                    ei[0] += 1
                    vo = vo_t[e]
                    cx.op("act" if e else "dve",
                          (lambda: nc.scalar.copy(out=vo[:n, :], in_=ps[pi][:n, :])) if e else
                          (lambda: nc.vector.tensor_copy(out=vo[:n, :], in_=ps[pi][:n, :])),
                          reads=[("ps", pi)], writes=[("vo", e)])
                    cx.dma("sp", VTM[s, p0:p0 + n, bi * 512:(bi + 1) * 512], vo[:n, :], reads=[("vo", e)], writes=[("VTM", s)])

            f_blocks = [(C_RG + 512 * i, AF.Silu) for i in range(2)] + [(C_GATES + 512 * i, AF.Sigmoid) for i in range(12)]
            go_t = [sc.sb([128, L], BF16) for _ in range(2)]
            gi = [0]
            for bi, (c0, fn) in enumerate(f_blocks):
                wt, wkey = next_w(c0)
                for j in range(4):
                    g = gi[0] % 2
                    gi[0] += 1
                    go = go_t[g]
                    for (q0, qn) in BLOCKS:
                        pi = psum_rot()
                        for kc in range(16):
                            cx.op("pe", lambda: nc.tensor.matmul(ps[pi][:, :qn], lhsT=wt[:, kc, j * 128:(j + 1) * 128], rhs=uT[:, kc, q0:q0 + qn],
                                                                 start=(kc == 0), stop=(kc == 15)),
                                  reads=UT_KEYS + [wkey], writes=[("ps", pi)])
                        cx.op("act", lambda: nc.scalar.activation(out=go[:, q0:q0 + qn], in_=ps[pi][:, :qn], func=fn),
                              reads=[("ps", pi)], writes=[("go", g)])
                    cx.dma("sp", GT[s, bi * 4 + j], go[:, :], reads=[("go", g)], writes=[("GT", s)])

    def stage_ret(l, s):
        with Scope(cx) as sc, nc.allow_non_contiguous_dma(reason="tiny param loads"):
            cr = sc.sb([128, RET_CW], F32, "c_ret_sb")
            cx.dma("sp", cr[:], c_ret[:, :], writes=["cr"])
            XF = cr[:, 0:2048]; XBr = cr[:, 2048:4096]
            X0f = cr[:, 4096:4224]; X0b = cr[:, 4224:4352]; Mge = cr[:, 4352:4480]; Mlt = cr[:, 4480:4608]
            XMK = cr[:, 4608:6656]; XMQ = cr[:, 6656:6912]
            dec = sc.sb([128, 16], F32); tt_ = sc.sb([128, 16], F32); pp = sc.sb([128, 16], F32); LG = sc.sb([128, 16], F32)
            cx.dma("sp", dec[:, 0:8], P["ret_log2_decay_f"][l].partition_broadcast(128), writes=["dec"])
            cx.dma("sp", dec[:, 8:16], P["ret_log2_decay_b"][l].partition_broadcast(128), writes=["dec"])
            cx.op("act", lambda: nc.scalar.activation(out=tt_[:], in_=dec[:], func=AF.Exp, scale=-math.log(2.0)), reads=["dec"], writes=["tt_"])
            cx.op("dve", lambda: nc.vector.tensor_scalar(out=pp[:], in0=tt_[:], scalar1=1.0 / 6.0, scalar2=0.2, op0=ALU.mult, op1=ALU.add),
                  reads=["tt_"], writes=["pp"])
            for cst in (0.25, 1.0 / 3.0, 0.5, 1.0):
                cx.op("dve", lambda: nc.vector.tensor_tensor(out=pp[:], in0=pp[:], in1=tt_[:], op=ALU.mult), reads=["pp", "tt_"], writes=["pp"])
                cx.op("dve", lambda: nc.vector.tensor_scalar(out=pp[:], in0=pp[:], scalar1=cst, scalar2=None, op0=ALU.add),
                      reads=["pp"], writes=["pp"])
            cx.op("dve", lambda: nc.vector.scalar_tensor_tensor(out=LG[:], in0=pp[:], scalar=-1.0, op0=ALU.mult, in1=tt_[:], op1=ALU.mult),
                  reads=["pp", "tt_"], writes=["LG"])
            gain = sc.sb([128, 8], F32)
            cx.dma("sp", gain[:], P["ret_out_gain"][l].rearrange("(h p) -> p h", p=128), writes=["gain"])
            TTs = [sc.sb([128, 33, 128], F32) for _ in range(2)]
            TMKs = [sc.sb([128, 16, 128], F32) for _ in range(2)]
            TMMs = [sc.sb([128, 16], F32) for _ in range(2)]
            TMQs = [sc.sb([128, 16, 16], F32) for _ in range(2)]
            E1 = sc.sb([128, 128], F32); E2 = sc.sb([128, 128], F32)
            KTs = [sc.sb([128, L], BF16) for _ in range(2)]
            QTs = [sc.sb([128, L], BF16) for _ in range(2)]
            Vs = [sc.sb([128, 17, 128], BF16) for _ in range(2)]
            for b in range(2):
                cx.op("pool", lambda: nc.gpsimd.memset(KTs[b][:], 0.0), writes=[("KT", b)])
                cx.op("pool", lambda: nc.gpsimd.memset(QTs[b][:], 0.0), writes=[("QT", b)])
                cx.op("pool", lambda: nc.gpsimd.memset(Vs[b][:, 0, :], 0.0), writes=[("V", b)])
                cx.op("pool", lambda: nc.gpsimd.memset(TMKs[b][:].rearrange("p a b -> p (a b)"), 0.0), writes=[("TT", b)])
                cx.op("pool", lambda: nc.gpsimd.memset(TMMs[b][:], 0.0), writes=[("TT", b)])
            SGs = [sc.sb([128, L], BF16) for _ in range(2)]
            YTs = [sc.sb([128, L], BF16) for _ in range(2)]
            Pts = [sc.sb([128, 512], BF16) for _ in range(4)]
            sqs = sc.sb([128, 512], F32); rss = sc.sb([128, 512], F32); t1s = sc.sb([128, 512], F32)
            Scp = [sc.sb([128, 512], F32) for _ in range(2)]
            sci = [0]
            pti = [0]

            def load_head(h):
                hb = h % 2
                TT, TMK, TMQ, KT, QT, V, SG = TTs[hb], TMKs[hb], TMQs[hb], KTs[hb], QTs[hb], Vs[hb], SGs[hb]
                TMM = TMMs[hb]
                kTT = ("TT", hb)
                lgf = LG[:, h:h + 1]; lgb = LG[:, 8 + h:9 + h]
                cx.dma("sp", KT[0:64, :], QKT[s, 4 + h // 2, (h % 2) * 64:(h % 2) * 64 + 64, :], writes=[("KT", hb)])
                cx.dma("sp", QT[0:64, :], QKT[s, h // 2, (h % 2) * 64:(h % 2) * 64 + 64, :], writes=[("QT", hb)])
                cx.dma("sp", V[:16, 0, :], VTM[s, 0:16, h * 128:(h + 1) * 128], writes=[("V", hb)])
                cx.dma("sp", V[:, 1:17, :], VTM[s, 16:L, h * 128:(h + 1) * 128].rearrange("(t p) c -> p t c", p=128), writes=[("V", hb)])
                cx.dma("sp", SG[:, :], GT[s, h], writes=[("SG", hb)])
                cx.op("act", lambda: nc.scalar.activation(out=TT[:, 17:33, :].rearrange("p a b -> p (a b)"), in_=XF, func=AF.Exp, scale=lgf),
                      reads=["cr", "LG"], writes=[kTT])
                cx.op("act", lambda: nc.scalar.activation(out=TT[:, 0:16, :].rearrange("p a b -> p (a b)"), in_=XBr, func=AF.Exp, scale=lgb),
                      reads=["cr", "LG"], writes=[kTT])
                cx.op("act", lambda: nc.scalar.activation(out=E1[:], in_=X0f, func=AF.Exp, scale=lgf), reads=["cr", "LG"], writes=["E1"])
                cx.op("act", lambda: nc.scalar.activation(out=E2[:], in_=X0b, func=AF.Exp, scale=lgb), reads=["cr", "LG"], writes=["E2"])
                cx.op("pool", lambda: nc.gpsimd.tensor_tensor(out=E1[:], in0=E1[:], in1=Mge, op=ALU.mult), reads=["E1", "cr"], writes=["E1"])
                cx.op("pool", lambda: nc.gpsimd.tensor_tensor(out=E2[:], in0=E2[:], in1=Mlt, op=ALU.mult), reads=["E2", "cr"], writes=["E2"])
                cx.op("pool", lambda: nc.gpsimd.tensor_tensor(out=TT[:, 16, :], in0=E1[:], in1=E2[:], op=ALU.add), reads=["E1", "E2"], writes=[kTT])
                cx.op("pool", lambda: nc.gpsimd.tensor_tensor(out=TMM[0:16, :], in0=E1[0:16, 0:16], in1=E2[0:16, 0:16], op=ALU.add), reads=["E1", "E2"], writes=[kTT])
                cx.op("act", lambda: nc.scalar.activation(out=TMK[0:16, :, :].rearrange("p a b -> p (a b)"), in_=XMK[0:16, :], func=AF.Exp, scale=lgf[0:16, :]),
                      reads=["cr", "LG"], writes=[kTT])
                cx.op("act", lambda: nc.scalar.activation(out=TMQ[:, :, :].rearrange("p a b -> p (a b)"), in_=XMQ, func=AF.Exp, scale=lgb),
                      reads=["cr", "LG"], writes=[kTT])

            pf = Prefetch([(lambda h=h: load_head(h)) for h in range(8)])
            pf.ensure(0)
            for h in range(8):
                hb = h % 2
                pf.ensure(h + 1)
                pump()
                TT, TMK, TMQ, KT, QT, V, SG, YTt = TTs[hb], TMKs[hb], TMQs[hb], KTs[hb], QTs[hb], Vs[hb], SGs[hb], YTs[hb]
                TMM = TMMs[hb]
                kTT = ("TT", hb)
                RT = [(0, 128)] + TILES[1:]
                for qi, (q0, N) in enumerate(BLOCKS):
                    po = 4 + (qi % 2)
                    pend = []

                    def emit_av(kt, M, pt):
                        cx.op("pe", lambda: nc.tensor.matmul(ps[po][:, :N], lhsT=V[:M, kt, :], rhs=Pts[pt][:M, :N], start=(kt == 0), stop=(kt == 16)),
                              reads=[("V", hb), ("Pt", pt)], writes=[("ps", po)])

                    for kt, (k0, M) in enumerate(RT):
                        pi = psum_rot(4)
                        cx.op("pe", lambda: nc.tensor.matmul(ps[pi][:M, :N], lhsT=KT[:, k0:k0 + M], rhs=QT[:, q0:q0 + N], start=True, stop=True),
                              reads=[("KT", hb), ("QT", hb)], writes=[("ps", pi)])
                        if kt == 0 and qi == 0:
                            mult = TMM[:, :]
                        elif kt == 0:
                            i0 = 4 * (qi - 1)
                            mult = TMK[:, i0:i0 + 4, :].rearrange("p a b -> p (a b)")
                        elif qi == 0:
                            mult = TMQ[:, kt - 1, :]
                        else:
                            i0 = 4 * (qi - 1); j = kt - 1
                            mult = TT[:, i0 - j + 16:i0 - j + 20, :].rearrange("p a b -> p (a b)")
                        pt = pti[0] % 4
                        pti[0] += 1
                        Pt = Pts[pt]
                        if False:
                            sb_ = sci[0] % 2
                            sci[0] += 1
                            cx.op("act", lambda: nc.scalar.copy(out=Scp[sb_][:M, :N], in_=ps[pi][:M, :N]), reads=[("ps", pi)], writes=[("Scp", sb_)])
                            cx.op("pool", lambda: nc.gpsimd.tensor_tensor(out=Pt[:M, :N], in0=Scp[sb_][:M, :N], in1=mult, op=ALU.mult),
                                  reads=[("Scp", sb_), kTT], writes=[("Pt", pt)])
                        else:
                            cx.op("dve", lambda: nc.vector.tensor_tensor(out=Pt[:M, :N], in0=ps[pi][:M, :N], in1=mult, op=ALU.mult),
                                  reads=[("ps", pi), kTT], writes=[("Pt", pt)])
                        pend.append((kt, M, pt))
                        if len(pend) > 2:
                            emit_av(*pend.pop(0))
                    while pend:
                        emit_av(*pend.pop(0))
                    cx.op("act", lambda: nc.scalar.activation(out=sqs[:, :N], in_=ps[po][:, :N], func=AF.Square), reads=[("ps", po)], writes=["sqs"])
                    pi = psum_rot(4)
                    cx.op("pe", lambda: nc.tensor.matmul(ps[pi][:, :N], lhsT=ones_f[:, :], rhs=sqs[:, :N], start=True, stop=True),
                          reads=["ones_f", "sqs"], writes=[("ps", pi)])
                    cx.op("act", lambda: nc.scalar.activation(out=rss[:, :N], in_=ps[pi][:, :N], func=AF.Ln, scale=1.0 / 128.0, bias=eps_t[:, 0:1]),
                          reads=[("ps", pi), "eps_t"], writes=["rss"])
                    cx.op("act", lambda: nc.scalar.activation(out=rss[:, :N], in_=rss[:, :N], func=AF.Exp, scale=-0.5), reads=["rss"], writes=["rss"])
                    cx.op("dve", lambda: nc.vector.tensor_tensor(out=t1s[:, :N], in0=ps[po][:, :N], in1=rss[:, :N], op=ALU.mult),
                          reads=[("ps", po), "rss"], writes=["t1s"])
                    cx.op("dve", lambda: nc.vector.scalar_tensor_tensor(out=YTt[:, q0:q0 + N], in0=t1s[:, :N], scalar=gain[:, h:h + 1], op0=ALU.mult,
                                                                        in1=SG[:, q0:q0 + N], op1=ALU.mult),
                          reads=["t1s", "gain", ("SG", hb)], writes=[("YTt", hb)])
                cx.dma("sp", YT[s, h], YTt[:, :], reads=[("YTt", hb)], writes=[("YT", s)])

    def stage_diff(l, s):
        lambda_init = 0.8 - 0.6 * math.exp(-0.3 * l)
        with Scope(cx) as sc, nc.allow_non_contiguous_dma(reason="tiny param loads"):
            lv = sc.sb([128, 4, 128], F32); lp = sc.sb([128, 2, 128], F32); ls = sc.sb([128, 4], F32); nlam = sc.sb([128, 1], F32)
            cx.dma("sp", lv[:].rearrange("p a b -> p (a b)"), P["diff_lambda"][l].rearrange("a b -> (a b)").partition_broadcast(128), writes=["lv"])
            cx.op("dve", lambda: nc.vector.tensor_tensor(out=lp[:, 0, :], in0=lv[:, 0, :], in1=lv[:, 1, :], op=ALU.mult), reads=["lv"], writes=["lp"])
            cx.op("dve", lambda: nc.vector.tensor_tensor(out=lp[:, 1, :], in0=lv[:, 2, :], in1=lv[:, 3, :], op=ALU.mult), reads=["lv"], writes=["lp"])
            cx.op("dve", lambda: nc.vector.tensor_reduce(out=ls[:, 0:2], in_=lp[:], op=ALU.add, axis=AX.X), reads=["lp"], writes=["ls"])
            cx.op("act", lambda: nc.scalar.activation(out=ls[:, 2:4], in_=ls[:, 0:2], func=AF.Exp), reads=["ls"], writes=["ls"])
            cx.op("dve", lambda: nc.vector.tensor_tensor(out=nlam[:], in0=ls[:, 3:4], in1=ls[:, 2:3], op=ALU.subtract), reads=["ls"], writes=["nlam"])
            cx.op("dve", lambda: nc.vector.tensor_scalar(out=nlam[:], in0=nlam[:], scalar1=-lambda_init, scalar2=None, op0=ALU.add),
                  reads=["nlam"], writes=["nlam"])
            gain = sc.sb([128, 2], F32)
            cx.dma("sp", gain[:], P["diff_out_gain"][l].rearrange("(c p) -> p c", p=128), writes=["gain"])
            cx.op("dve", lambda: nc.vector.tensor_scalar(out=gain[:], in0=gain[:], scalar1=1.0 - lambda_init, scalar2=None, op0=ALU.mult),
                  reads=["gain"], writes=["gain"])
            KTs = [sc.sb([128, 2, L], BF16) for _ in range(2)]
            QTs = [sc.sb([128, 2, L], BF16) for _ in range(2)]
            Vs = [sc.sb([128, 17, 256], BF16) for _ in range(2)]
            for b in range(2):
                cx.op("pool", lambda: nc.gpsimd.memset(Vs[b][:, 0, :], 0.0), writes=[("V", b)])
            YTs = [sc.sb([128, 2, L], BF16) for _ in range(2)]
            RT = [(0, 128)] + TILES[1:]
            Pts = [sc.sb([128, 512], BF16) for _ in range(4)]
            ON = sc.sb([128, 2, 2, 512], F32)
            rcp = sc.sb([128, 512], F32); osb = sc.sb([128, 2, 512], F32); sqs = sc.sb([128, 2, 512], F32); rss = sc.sb([128, 512], F32)
            pti = [0]
            scale = 128.0 ** -0.5
            import os
            CUT = int(os.environ.get("DIFF_CUT", "9"))
            def load_head(h):
                hb = h % 2
                KT, QT, V = KTs[hb], QTs[hb], Vs[hb]
                cx.dma("sp", KT[:, :, :], QKT[s, 16 + 2 * h:18 + 2 * h].rearrange("c p t -> p c t"), writes=[("KT", hb)])
                cx.dma("sp", QT[:, :, :], QKT[s, 8 + 2 * h:10 + 2 * h].rearrange("c p t -> p c t"), writes=[("QT", hb)])
                cx.dma("sp", V[:16, 0, :], VTM[s, 0:16, 1024 + h * 256:1024 + (h + 1) * 256], writes=[("V", hb)])
                for tg in range(2):
                    cx.dma("sp", V[:, 1 + 8 * tg:9 + 8 * tg, :],
                           VTM[s, 16 + 1024 * tg:16 + 1024 * (tg + 1), 1024 + h * 256:1024 + (h + 1) * 256].rearrange("(t p) c -> p t c", p=128),
                           writes=[("V", hb)])

            pf = Prefetch([(lambda h=h: load_head(h)) for h in range(4)])
            pf.ensure(0)
            for h in range(4 if CUT > 0 else 0):
                hb = h % 2
                pf.ensure(h + 1)
                pump()
                KT, QT, V, YTt = KTs[hb], QTs[hb], Vs[hb], YTs[hb]
                for qi, (q0, N) in enumerate(BLOCKS if CUT > 1 else []):
                    for half in range(2):
                        pend = []

                        def emit_av(kt, M, pt):
                            Pt = Pts[pt]
                            for c in range(2):
                                cx.op("pe", lambda: nc.tensor.matmul(ps[3 + c][:, :N], lhsT=V[:M, kt, c * 128:(c + 1) * 128], rhs=Pt[:M, :N],
                                                                     start=(kt == 0), stop=(kt == 16)),
                                      reads=[("V", hb), ("Pt", pt)], writes=[("ps", 3 + c)])
                            cx.op("pe", lambda: nc.tensor.matmul(ps[5][:, :N], lhsT=(ones_m if kt == 0 else ones_b)[:M, :], rhs=Pt[:M, :N], start=(kt == 0), stop=(kt == 16)),
                                  reads=["ones_b", "ones_m", ("Pt", pt)], writes=[("ps", 5)])

                        for kt, (k0, M) in enumerate(RT):
                            pi = psum_rot(3)
                            cx.op("pe", lambda: nc.tensor.matmul(ps[pi][:M, :N], lhsT=KT[:, half, k0:k0 + M], rhs=QT[:, half, q0:q0 + N], start=True, stop=True),
                                  reads=[("KT", hb), ("QT", hb)], writes=[("ps", pi)])
                            pt = pti[0] % 4
                            pti[0] += 1
                            Pt = Pts[pt]
                            cx.op("act", lambda: nc.scalar.activation(out=Pt[:M, :N], in_=ps[pi][:M, :N], func=AF.Exp, scale=scale),
                                  reads=[("ps", pi)], writes=[("Pt", pt)])
                            pend.append((kt, M, pt))
                            if len(pend) > 2:
                                emit_av(*pend.pop(0))
                        while pend:
                            emit_av(*pend.pop(0))
                        if CUT <= 2:
                            continue
                        cx.op("dve", lambda: nc.vector.reciprocal(out=rcp[:, :N], in_=ps[5][:, :N]), reads=[("ps", 5)], writes=["rcp"])
                        for c in range(2):
                            cx.op("dve", lambda: nc.vector.tensor_tensor(out=ON[:, half, c, :N], in0=ps[3 + c][:, :N], in1=rcp[:, :N], op=ALU.mult),
                                  reads=[("ps", 3 + c), "rcp"], writes=[("ON", half)])
                    if CUT <= 3:
                        continue
                    for c in range(2):
                        cx.op("dve", lambda: nc.vector.scalar_tensor_tensor(out=osb[:, c, :N], in0=ON[:, 1, c, :N], scalar=nlam[:, 0:1], op0=ALU.mult,
                                                                            in1=ON[:, 0, c, :N], op1=ALU.add),
                              reads=[("ON", 0), ("ON", 1), "nlam"], writes=["osb"])
                        cx.op("pool", lambda: nc.gpsimd.tensor_tensor(out=sqs[:, c, :N], in0=osb[:, c, :N], in1=osb[:, c, :N], op=ALU.mult),
                              reads=["osb"], writes=["sqs"])
                    if CUT <= 5:
                        continue
                    pi = psum_rot(3)
                    for c in range(2):
                        cx.op("pe", lambda: nc.tensor.matmul(ps[pi][:, :N], lhsT=ones_f[:, :], rhs=sqs[:, c, :N], start=(c == 0), stop=(c == 1)),
                              reads=["ones_f", "sqs"], writes=[("ps", pi)])
                    cx.op("act", lambda: nc.scalar.activation(out=rss[:, :N], in_=ps[pi][:, :N], func=AF.Sqrt, scale=1.0 / 256.0, bias=eps_t[:, 0:1]),
                          reads=[("ps", pi), "eps_t"], writes=["rss"])
                    cx.op("dve", lambda: nc.vector.reciprocal(out=rss[:, :N], in_=rss[:, :N]), reads=["rss"], writes=["rss"])
                    if CUT <= 6:
                        continue
                    for c in range(2):
                        cx.op("dve", lambda: nc.vector.tensor_tensor(out=sqs[:, c, :N], in0=osb[:, c, :N], in1=rss[:, :N], op=ALU.mult),
                              reads=["osb", "rss"], writes=["sqs"])
                        cx.op("dve", lambda: nc.vector.tensor_scalar(out=YTt[:, c, q0:q0 + N], in0=sqs[:, c, :N], scalar1=gain[:, c:c + 1], scalar2=None,
                                                                     op0=ALU.mult),
                              reads=["sqs", "gain"], writes=[("YTt", hb)])
                cx.dma("sp", YT[s, 8 + 2 * h:10 + 2 * h].rearrange("c p t -> p c t"), YTt[:, :, :], reads=[("YTt", hb)], writes=[("YT", s)])

    def stage_na(l, s):
        with Scope(cx) as sc:
            cm = sc.sb([128, 64], F32, "cm")
            cx.dma("sp", cm[:], c_cm[:, :], writes=["cm"])
            bts = [[sc.sb([128, 15, 64], F32) for _ in range(2)] for _ in range(2)]
            EBf = [[sc.sb([128, 16, 64], BF16) for _ in range(2)] for _ in range(2)]
            EBi = [[sc.sb([128, 24, 64], BF16) for _ in range(2)] for _ in range(2)]
            KTs = [[sc.sb([128, L], BF16) for _ in range(2)] for _ in range(2)]
            QTs = [sc.sb([128, L], BF16) for _ in range(2)]
            Vz = [[sc.sb([128, 17, 128], BF16) for _ in range(2)] for _ in range(2)]
            for x in range(2):
                for b in range(2):
                    cx.op("pool", lambda: nc.gpsimd.memset(EBi[x][b][:], 0.0), writes=[("EB", x, b)])
                    cx.op("pool", lambda: nc.gpsimd.memset(EBf[x][b][:], 0.0), writes=[("EB", x, b)])
                    cx.op("pool", lambda: nc.gpsimd.memset(Vz[x][b][:], 0.0), writes=[("Vz", x, b)])
                    cx.op("pool", lambda: nc.gpsimd.memset(KTs[x][b][:], 0.0), writes=[("KT", x, b)])
            oh = [sc.sb([128, 128], BF16) for _ in range(2)]
            for b in range(2):
                cx.op("pool", lambda: nc.gpsimd.memset(oh[b][:], 0.0), writes=["oh"])
                cx.op("pool", lambda: nc.gpsimd.memset(oh[b][:, b * 64:(b + 1) * 64], 1.0), writes=["oh"])
            ohm = [sc.sb([128, 128], BF16) for _ in range(2)]
            for b in range(2):
                cx.op("pool", lambda: nc.gpsimd.memset(ohm[b][:], 0.0), writes=["oh"])
                cx.op("pool", lambda: nc.gpsimd.memset(ohm[b][0:16, b * 64:(b + 1) * 64], 1.0), writes=["oh"])
            YTs = [sc.sb([128, L], BF16) for _ in range(2)]
            Ets = [sc.sb([128, 512], BF16) for _ in range(5)]
            Pts = [sc.sb([128, 512], BF16) for _ in range(5)]
            rcp = sc.sb([128, 512], F32)
            pti = [0]
            QB = [(0, 4, list(range(0, 4)), "f")] + [(4 + 8 * g, 8, list(range(4 * g, 4 * g + 8)), "i") for g in range(3)] + [(28, 4, list(range(12, 16)), "f")]

            def load_pair(hp):
                x = hp % 2
                cx.dma("sp", QTs[x][:, :], QKT[s, 24 + hp], writes=[("QT", x)])
                for par in range(2):
                    h = 2 * hp + par
                    KT, V = KTs[x][par], Vz[x][par]
                    bt, ebf, ebi = bts[x][par], EBf[x][par], EBi[x][par]
                    kEB = ("EB", x, par)
                    cx.dma("sp", bt[:].rearrange("p a b -> p (a b)"), na_bt[l, h], writes=[("bt", x, par)])
                    cx.dma("sp", KT[par * 64:par * 64 + 64, :], QKT[s, 32 + hp, par * 64:par * 64 + 64, :], writes=[("KT", x, par)])
                    cx.dma("sp", V[:16, 0, par * 64:par * 64 + 64], VTM[s, 0:16, 2048 + h * 64:2048 + (h + 1) * 64], writes=[("Vz", x, par)])
                    for tg in range(4):
                        cx.dma("sp", V[:, 1 + 4 * tg:5 + 4 * tg, par * 64:par * 64 + 64],
                               VTM[s, 16 + 512 * tg:16 + 512 * (tg + 1), 2048 + h * 64:2048 + (h + 1) * 64].rearrange("(t p) c -> p t c", p=128),
                               writes=[("Vz", x, par)])
                    cx.op("act", lambda: nc.scalar.activation(out=bt[:], in_=bt[:], func=AF.Exp), reads=[("bt", x, par)], writes=[("bt", x, par)])
                    cx.op("dve", lambda: nc.vector.tensor_tensor(out=ebf[0:64, 0:15, :], in0=bt[0:64], in1=cm[0:64, :].unsqueeze(1).to_broadcast([64, 15, 64]), op=ALU.mult),
                          reads=[("bt", x, par), "cm"], writes=[kEB])
                    cx.op("pool", lambda: nc.gpsimd.tensor_tensor(out=ebf[64:128, 1:16, :], in0=bt[64:128], in1=cm[64:128, :].unsqueeze(1).to_broadcast([64, 15, 64]), op=ALU.mult),
                          reads=[("bt", x, par), "cm"], writes=[kEB])
                    cx.op("pool", lambda: nc.gpsimd.tensor_copy(out=ebi[0:64, 8:16, :], in_=ebf[0:64, 4:12, :]), reads=[kEB], writes=[kEB])
                    cx.op("pool", lambda: nc.gpsimd.tensor_copy(out=ebi[64:128, 9:17, :], in_=ebf[64:128, 5:13, :]), reads=[kEB], writes=[kEB])

            pf = Prefetch([(lambda hp=hp: load_pair(hp)) for hp in range(8)])
            pf.ensure(0)
            for hp in range(8):
                x = hp % 2
                pf.ensure(hp + 1)
                pump()
                YTt = YTs[x]
                blocks = [(0, 16, None, None, None)] + [(16 + 64 * r0, 64 * R, kts, kind, r0) for (r0, R, kts, kind) in QB]
                for (q0, N, kts, kind, r0) in blocks:
                    items = []
                    for par in range(2):
                        klist = [(-1, 0, 128)] + ([(j, 16 + 128 * j, 128) for j in kts] if kts is not None else [])
                        for (j, k0, M) in klist:
                            items.append((par, j, k0, M))
                    pend = []

                    def emit_av(idx, par, j, M, pt):
                        V = Vz[x][par]
                        Pt = Pts[pt]
                        first = (idx == 0); last = (idx == len(items) - 1)
                        cx.op("pe", lambda: nc.tensor.matmul(ps[4][:, :N], lhsT=V[:M, j + 1, :], rhs=Pt[:M, :N], start=first, stop=last),
                              reads=[("Vz", x, par), ("Pt", pt)], writes=[("ps", 4)])
                        cx.op("pe", lambda: nc.tensor.matmul(ps[5][:, :N], lhsT=(ohm if j < 0 else oh)[par][:M, :], rhs=Pt[:M, :N], start=first, stop=last),
                              reads=["oh", ("Pt", pt)], writes=[("ps", 5)])

                    for idx, (par, j, k0, M) in enumerate(items):
                        KT, QT = KTs[x][par], QTs[x]
                        ebf, ebi = EBf[x][par], EBi[x][par]
                        kEB = ("EB", x, par)
                        pi = psum_rot(4)
                        cx.op("pe", lambda: nc.tensor.matmul(ps[pi][:M, :N], lhsT=KT[:, k0:k0 + M], rhs=QT[:, q0:q0 + N], start=True, stop=True),
                              reads=[("KT", x, par), ("QT", x)], writes=[("ps", pi)])
                        pt = pti[0] % 5
                        pti[0] += 1
                        Et, Pt = Ets[pt], Pts[pt]
                        if j < 0:
                            cx.op("act", lambda: nc.scalar.activation(out=Pt[:M, :N], in_=ps[pi][:M, :N], func=AF.Exp, scale=0.125),
                                  reads=[("ps", pi)], writes=[("Pt", pt)])
                        else:
                            cx.op("act", lambda: nc.scalar.activation(out=Et[:M, :N], in_=ps[pi][:M, :N], func=AF.Exp, scale=0.125),
                                  reads=[("ps", pi)], writes=[("Et", pt)])
                            R = N // 64
                            e0 = 7 - 2 * j + r0
                            if kind == "f":
                                assert 1 <= e0 and e0 + R <= 15, (e0, R)
                                tab = ebf[:, e0:e0 + R, :]
                            else:
                                i0 = e0 + 4
                                assert 1 <= i0 and i0 + R <= 23, (i0, R)
                                tab = ebi[:, i0:i0 + R, :]
                            tab = tab.rearrange("p a b -> p (a b)")
                            cx.op("dve", lambda: nc.vector.tensor_tensor(out=Pt[:, :N], in0=Et[:, :N], in1=tab, op=ALU.mult),
                                  reads=[("Et", pt), kEB], writes=[("Pt", pt)])
                        pend.append((idx, par, j, M, pt))
                        if len(pend) > 3:
                            emit_av(*pend.pop(0))
                    while pend:
                        emit_av(*pend.pop(0))
                    cx.op("dve", lambda: nc.vector.reciprocal(out=rcp[:, :N], in_=ps[5][:, :N]), reads=[("ps", 5)], writes=["rcp"])
                    cx.op("dve", lambda: nc.vector.tensor_tensor(out=YTt[:, q0:q0 + N], in0=ps[4][:, :N], in1=rcp[:, :N], op=ALU.mult),
                          reads=[("ps", 4), "rcp"], writes=[("YTt", x)])
                cx.dma("sp", YT[s, 16 + hp], YTt[:, :], reads=[("YTt", x)], writes=[("YT", s)])

    def stage_merge(l, s, src):
        with Scope(cx) as sc:
            Y = sc.sb([128, 24, L], BF16, "Yall")
            for c3 in range(6):
                cx.dma("sp", Y[:, 4 * c3:4 * c3 + 4, :], YT[s, 4 * c3:4 * c3 + 4].rearrange("c p t -> p c t"), writes=["Y"])
            wbs = [sc.sb([128, 24, 512], BF16) for _ in range(2)]
            gts = [sc.sb([128, 3, L], BF16) for _ in range(2)]
            mts = [sc.sb([128, L], BF16) for _ in range(2)]
            macc = [sc.sb([128, 512], F32) for _ in range(2)]
            mtmp = [sc.sb([128, 512], F32) for _ in range(2)]
            wnames = ["w_br_ret", "w_br_diff", "w_br_na"]
            oi = [0]

            def load_wb(og):
                for b in range(3):
                    cx.dma("sp", wbs[og % 2][:, 8 * b:8 * b + 8, :], WB[wnames[b]][l][:, og * 512:(og + 1) * 512].rearrange("(k p) c -> p k c", p=128),
                           reads=[("WB", wnames[b], l)], writes=[("wb", og % 2)])

            def load_gt(oc):
                for b in range(3):
                    cx.dma("sp", gts[oc % 2][:, b, :], GT[s, 8 + 16 * b + oc], writes=[("gt", oc % 2)])

            pf_wb = Prefetch([(lambda og=og: load_wb(og)) for og in range(4)])
            pf_gt = Prefetch([(lambda oc=oc: load_gt(oc)) for oc in range(16)])
            pf_wb.ensure(0)
            pf_gt.ensure(0)
            for og in range(4):
                wb = wbs[og % 2]
                pf_wb.ensure(og + 1)
                pump()
                for j in range(4):
                    oc = og * 4 + j
                    o = oi[0] % 2
                    oi[0] += 1
                    gt, mt = gts[o], mts[o]
                    pf_gt.ensure(oc + 1)
                    for bi, (q0, N) in enumerate(BLOCKS):
                        m = bi % 2
                        for b in range(3):
                            pi = psum_rot()
                            for kc in range(8):
                                cx.op("pe", lambda: nc.tensor.matmul(ps[pi][:, :N], lhsT=wb[:, 8 * b + kc, j * 128:(j + 1) * 128], rhs=Y[:, 8 * b + kc, q0:q0 + N],
                                                                     start=(kc == 0), stop=(kc == 7)),
                                      reads=[("wb", og % 2), "Y"], writes=[("ps", pi)])
                            dst = macc[m] if b == 0 else mtmp[m]
                            dk = ("macc", m) if b == 0 else ("mtmp", m)
                            cx.op("dve", lambda: nc.vector.tensor_tensor(out=dst[:, :N], in0=ps[pi][:, :N], in1=gt[:, b, q0:q0 + N], op=ALU.mult),
                                  reads=[("ps", pi), ("gt", o)], writes=[dk])
                            if b == 1:
                                cx.op("pool", lambda: nc.gpsimd.tensor_tensor(out=macc[m][:, :N], in0=macc[m][:, :N], in1=mtmp[m][:, :N], op=ALU.add),
                                      reads=[("macc", m), ("mtmp", m)], writes=[("macc", m)])
                            if b == 2:
                                cx.op("pool", lambda: nc.gpsimd.tensor_tensor(out=mt[:, q0:q0 + N], in0=macc[m][:, :N], in1=mtmp[m][:, :N], op=ALU.add),
                                      reads=[("macc", m), ("mtmp", m)], writes=[("mt", o)])
                    cx.dma("sp", MT[s, oc], mt[:, :], reads=[("mt", o)], writes=[("MT", s)])
        with Scope(cx) as sc:
            M_ = sc.sb([128, 16, L], BF16, "Mall")
            for c4 in range(4):
                cx.dma("sp", M_[:, 4 * c4:4 * c4 + 4, :], MT[s, 4 * c4:4 * c4 + 4].rearrange("c p t -> p c t"), writes=["M_"])
            wos = [sc.sb([128, 16, 512], BF16) for _ in range(2)]
            hts = [sc.sb([128, 512], F32) for _ in range(3)]
            hi = [0]
            pf_wo = Prefetch([(lambda cb=cb: load_w_block(wos[cb % 2], WB["w_out"][l], cb * 512, 512, (("WB", "w_out", l), ("wo", cb % 2))))
                              for cb in range(4)])
            pf_wo.ensure(0)
            for cb in range(4):
                wo = wos[cb % 2]
                pf_wo.ensure(cb + 1)
                for ti, (p0, n) in enumerate(TILES):
                    pi = psum_rot()
                    for kc in range(16):
                        cx.op("pe", lambda: nc.tensor.matmul(ps[pi][:n, :], lhsT=M_[:, kc, p0:p0 + n], rhs=wo[:, kc, :], start=(kc == 0), stop=(kc == 15)),
                              reads=["M_", ("wo", cb % 2)], writes=[("ps", pi)])
                    hh = hi[0] % 3
                    hi[0] += 1
                    ht = hts[hh]
                    cx.dma("sp", ht[:n, :], src[s, p0:p0 + n, cb * 512:(cb + 1) * 512], reads=[("hsrc", s, ti, cb)], writes=[("ht", hh)])
                    cx.op("dve", lambda: nc.vector.tensor_tensor(out=ht[:n, :], in0=ps[pi][:n, :], in1=ht[:n, :], op=ALU.add),
                          reads=[("ps", pi), ("ht", hh)], writes=[("ht", hh)])
                    cx.dma("sp", hbuf[s, p0:p0 + n, cb * 512:(cb + 1) * 512], ht[:n, :], reads=[("ht", hh)], writes=[("hsrc", s, ti, cb)])

    FBLOCKS = [(510 * b, min(510, L - 510 * b)) for b in range(5)]

    def stage_ffn(l, s):
        with Scope(cx) as sc:
            uT = sc.sb([128, 16, L], BF16, "uT2")
            with Scope(cx) as sc2:
                norm_to_uT(sc2, hbuf, s, P["norm_ffn"][l], uT)
            cwT = sc.sb([43, 4, 128], F32); cwb = sc.sb([128, 4, 64], F32)
            cx.dma("sp", cwT[:, 0:3, :], P["ffn_conv_w"][l].rearrange("j (f p) -> f j p", p=128), writes=["cwT"])
            cx.dma("sp", cwT[:, 3, :], P["ffn_conv_b"][l].rearrange("(f p) -> f p", p=128), writes=["cwT"])
            pi = psum_rot()
            for j in range(4):
                cx.op("pe", lambda: nc.tensor.transpose(out=ps[pi][:, j * 64:j * 64 + 43], in_=cwT[:, j, :], identity=ident_f[:43, :43]),
                      reads=["cwT", "ident_f"], writes=[("ps", pi)])
            cx.op("dve", lambda: nc.vector.tensor_copy(out=cwb[:].rearrange("p a b -> p (a b)"), in_=ps[pi][:, 0:256]), reads=[("ps", pi)], writes=["cwb"])
            HA = sc.sb([128, 43, 512], BF16, "HA")
            wus = [sc.sb([128, 16, 256], BF16) for _ in range(2)]
            wds = [sc.sb([128, 43, 256], BF16) for _ in range(2)]
            gs = [sc.sb([128, 514], F32) for _ in range(2)]
            acc = [sc.sb([128, 512], F32) for _ in range(2)]
            sgt = [sc.sb([128, 512], F32) for _ in range(2)]
            hts = [sc.sb([128, 256], F32) for _ in range(3)]
            for b in range(2):
                cx.op("pool", lambda: nc.gpsimd.memset(gs[b][:], 0.0), writes=[("gs", b)])
            wi = [0]; di = [0]; hi = [0]

            def load_wd(i):
                nb = i % 8
                cx.dma("sp", wds[i % 2][:, :, :], WD2[l, nb], reads=[("WD2", l)], writes=[("wd", i % 2)])

            pf_wu = Prefetch([(lambda i=i: cx.dma("sp", wus[i % 2][:, :, :], WU[l, i % 43], reads=[("WU", l)], writes=[("wu", i % 2)]))
                              for i in range(43 * len(FBLOCKS))])
            pf_wd = Prefetch([(lambda i=i: load_wd(i)) for i in range(8 * len(FBLOCKS))])
            pf_wu.ensure(0)
            pf_wd.ensure(0)
            for bi, (c0, n) in enumerate(FBLOCKS):
                pump(16)
                w0 = max(c0 - 1, 0); w1 = min(c0 + n + 1, L); NW = w1 - w0
                goff = w0 - (c0 - 1)
                for f in range(43):
                    wb_ = wi[0] % 2
                    pf_wu.ensure(wi[0] + 1)
                    wi[0] += 1
                    wu = wus[wb_]
                    pg = psum_rot(); pv = psum_rot()
                    for half, pp_ in ((0, pg), (1, pv)):
                        for kc in range(16):
                            cx.op("pe", lambda: nc.tensor.matmul(ps[pp_][:, :NW], lhsT=wu[:, kc, half * 128:(half + 1) * 128], rhs=uT[:, kc, w0:w1],
                                                                 start=(kc == 0), stop=(kc == 15)),
                                  reads=UT_KEYS + [("wu", wb_)], writes=[("ps", pp_)])
                    g = f % 2
                    G, A, SGt = gs[g], acc[g], sgt[g]
                    if goff > 0 or bi == len(FBLOCKS) - 1:
                        cx.op("pool", lambda: nc.gpsimd.memset(G[:], 0.0), writes=[("gs", g)])
                    cx.op("act", lambda: nc.scalar.copy(out=G[:, goff:goff + NW], in_=ps[pg][:, :NW]), reads=[("ps", pg)], writes=[("gs", g)])
                    cx.op("dve", lambda: nc.vector.tensor_scalar(out=A[:, :n], in0=G[:, 0:n], scalar1=cwb[:, 0, f:f + 1], scalar2=None, op0=ALU.mult),
                          reads=[("gs", g), "cwb"], writes=[("acc", g)])
                    for j in (1, 2):
                        cx.op("dve", lambda: nc.vector.scalar_tensor_tensor(out=A[:, :n], in0=G[:, j:j + n], scalar=cwb[:, j, f:f + 1], op0=ALU.mult,
                                                                            in1=A[:, :n], op1=ALU.add),
                              reads=[("gs", g), "cwb", ("acc", g)], writes=[("acc", g)])
                    cx.op("act", lambda: nc.scalar.activation(out=SGt[:, :n], in_=A[:, :n], func=AF.Silu, bias=cwb[:, 3, f:f + 1]),
                          reads=[("acc", g), "cwb"], writes=[("sgt", g)])
                    voff = c0 - w0
                    cx.op("dve", lambda: nc.vector.tensor_tensor(out=HA[:, f, :n], in0=ps[pv][:, voff:voff + n], in1=SGt[:, :n], op=ALU.mult),
                          reads=[("ps", pv), ("sgt", g)], writes=[("HA", f)])
                HA_KEYS = [("HA", f) for f in range(43)]
                for nb in range(8):
                    db = di[0] % 2
                    pf_wd.ensure(di[0] + 1)
                    di[0] += 1
                    wd = wds[db]
                    for t0 in range(0, n, 128):
                        tn = min(128, n - t0)
                        pi = psum_rot()
                        for f in range(43):
                            cx.op("pe", lambda: nc.tensor.matmul(ps[pi][:tn, :256], lhsT=HA[:, f, t0:t0 + tn], rhs=wd[:, f, :], start=(f == 0), stop=(f == 42)),
                                  reads=HA_KEYS + [("wd", db)], writes=[("ps", pi)])
                        hh = hi[0] % 3
                        hi[0] += 1
                        ht = hts[hh]
                        r0 = c0 + t0
                        cx.dma("sp", ht[:tn, :], hbuf[s, r0:r0 + tn, nb * 256:(nb + 1) * 256], reads=[("hb", s, bi)], writes=[("ht", hh)])
                        cx.op("dve", lambda: nc.vector.tensor_tensor(out=ht[:tn, :], in0=ps[pi][:tn, :256], in1=ht[:tn, :], op=ALU.add),
                              reads=[("ps", pi), ("ht", hh)], writes=[("ht", hh)])
                        cx.dma("sp", hbuf[s, r0:r0 + tn, nb * 256:(nb + 1) * 256], ht[:tn, :], reads=[("ht", hh)], writes=[("hb2", s, bi)])

    eps_t = cs.sb([128, 1], F32, "eps_t")
    cx.op("dve", lambda: nc.vector.memset(eps_t[:], EPS), writes=["eps_t"])
    cx.barrier()

    for l in range(nlayers):
        src = xin if l == 0 else hbuf
        if l >= 1:
            pump(len(cast_q))
        for s in range(nseq):
            if "proj" in stages:
                stage_proj(l, s, src)
            if "ret" in stages:
                stage_ret(l, s)
            if "diff" in stages:
                stage_diff(l, s)
            if "na" in stages:
                stage_na(l, s)
            if "merge" in stages:
                stage_merge(l, s, src)
            if "ffn" in stages:
                stage_ffn(l, s)

    if "merge" not in stages:
        for s in range(nseq):
            cx.dma("sp", hbuf[s], xin[s], writes=[("hbuf", s)])

    cx.barrier()
    cx.finish()
    cs.es.__exit__(None, None, None)
    cx.es.close()
    return nc


def host_consts():
    c = {}
    c["c_ident"] = np.eye(128, dtype=np.float32)
    pos = np.arange(L, dtype=np.float32)
    for name, d in (("c_rope_r", 64), ("c_rope_d", 128)):
        inv = np.power(np.float32(10000.0), -np.arange(0, d, 2, dtype=np.float32) / np.float32(d)).astype(np.float32)
        ang = pos[:, None] * inv[None, :]
        c[name] = np.stack([np.cos(ang), np.sin(ang)], axis=1).astype(np.float32)
    a = np.arange(128, dtype=np.float32)[None, :]
    b = np.arange(128, dtype=np.float32)[:, None]
    cr = np.zeros((128, RET_CW), np.float32)
    XF = np.stack([128.0 * d + a - b for d in range(1, 17)], axis=1)
    XBr = np.stack([128.0 * (16 - dd) + b - a for dd in range(16)], axis=1)
    cr[:, 0:2048] = XF.reshape(128, 2048)
    cr[:, 2048:4096] = XBr.reshape(128, 2048)
    cr[:, 4096:4224] = a - b
    cr[:, 4224:4352] = b - a
    cr[:, 4352:4480] = (a >= b)
    cr[:, 4480:4608] = (a < b)
    XMK = np.stack([16.0 + 128.0 * i + a - b for i in range(16)], axis=1)
    cr[:, 4608:6656] = XMK.reshape(128, 2048)
    a16 = np.arange(16, dtype=np.float32)[None, :]
    XMQ = np.stack([16.0 + 128.0 * j + b - a16 for j in range(16)], axis=1)
    cr[:, 6656:6912] = XMQ.reshape(128, 256)
    c["c_ret"] = cr
    kc = (np.arange(128) % 64)[:, None]
    qc = np.arange(64)[None, :]
    cstart = np.clip(qc - 8, 0, 48)
    c["c_cm"] = ((kc >= cstart) & (kc < cstart + 16)).astype(np.float32)
    return c


def na_bias_table(rpb):
    kc = (np.arange(128) % 64)[:, None]
    qc = np.arange(64)[None, :]
    dc = np.clip(kc - qc + 15, 0, 30)
    e = np.arange(15)
    bt = rpb[:, :, (14 - e)[None, :, None], dc[:, None, :]]
    return np.ascontiguousarray(bt.reshape(rpb.shape[0], 16, 128, 960).astype(np.float32))


WNAMES = ["w_in", "w_br_ret", "w_br_diff", "w_br_na", "w_out", "w_up", "w_down"]
SNAMES = ["norm_mix", "norm_ffn", "ret_log2_decay_f", "ret_log2_decay_b", "ret_out_gain", "diff_q_gain", "diff_k_gain",
          "diff_lambda", "diff_out_gain", "na_q_gain", "na_k_gain", "ffn_conv_w", "ffn_conv_b"]


def make_in_maps(inputs, nseq, ncores):
    x = np.asarray(inputs["x"], dtype=np.float32)
    meta = np.asarray(inputs["meta_tokens"], dtype=np.float32)
    consts = host_consts()
    shared = {k: np.ascontiguousarray(np.asarray(inputs[k], dtype=np.float32)) for k in WNAMES + SNAMES}
    shared.update(consts)
    shared["na_bt"] = na_bias_table(np.asarray(inputs["na_rpb"], dtype=np.float32))
    maps = []
    for c in range(ncores):
        xin = np.empty((nseq, L, D), np.float32)
        for s in range(nseq):
            xin[s, :NMETA] = meta
            xin[s, NMETA:] = x[c * nseq + s]
        m = dict(shared)
        m["xin"] = xin
        maps.append(m)
    return maps


def kernel(**inputs):
    nseq = 2
    nc = build_program(nseq=nseq)
    maps = make_in_maps(inputs, nseq, NCORES)
    res = run_bass_kernel_spmd(nc, maps, core_ids=list(range(NCORES)))
    out = np.empty((16, SEQ, D), np.float32)
    for c in range(NCORES):
        hb = res.results[c]["hbuf"]
        for s in range(nseq):
            out[c * nseq + s] = hb[s, NMETA:]
    return out
```

```python
import contextlib
import math
import numpy as np
import concourse.bass as bass
import concourse.mybir as mybir
from concourse.bass_utils import run_bass_kernel_spmd

F32 = mybir.dt.float32
BF16 = mybir.dt.bfloat16
AF = mybir.ActivationFunctionType
ALU = mybir.AluOpType
AX = mybir.AxisListType

D = 2048
SEQ = 2048
NMETA = 16
L = SEQ + NMETA
DEPTH = 2
INW = 15360
DFF = 5504
EPS = 1e-6
NCORES = 8
RET_CW = 6912
ALL_STAGES = ('proj', 'ret', 'diff', 'na', 'merge', 'ffn')

TILES = [(0, 16)] + [(16 + 128 * t, 128) for t in range(16)]
BLOCKS = [(0, 16)] + [(16 + 512 * b, 512) for b in range(4)]

C_RQ, C_RK, C_RV, C_RG = 0, 512, 1024, 2048
C_DQ, C_DK, C_DV = 3072, 4096, 5120
C_NQ, C_NK, C_NV = 6144, 7168, 8192
C_GATES = 9216


class Ctx:
    NDMA = 20

    def __init__(self, nc):
        self.nc = nc
        self.es = contextlib.ExitStack()
        self.eng = {"pe": nc.tensor, "act": nc.scalar, "dve": nc.vector, "pool": nc.gpsimd, "sp": nc.sync}
        self.sem = {}
        self.cnt = {}
        self.seen = {e: {} for e in self.eng}
        for e in self.eng:
            self.sem[e] = self.es.enter_context(nc.semaphore("s_" + e))
            self.cnt[e] = 0
        self.dslots = {}
        self.dnext = {}
        for q in ("sp", "pool", "act"):
            n = self.NDMA
            self.dslots[q] = [[self.es.enter_context(nc.semaphore("d_%s%d" % (q, i))), 0] for i in range(n)]
            self.dnext[q] = 0
        self.lastw = {}
        self.readers = {}
        self.nwaits = 0

    def _handle(self, key):
        if isinstance(key, str):
            return self.sem[key]
        return self.dslots[key[0]][key[1]][0]

    def _wait(self, e, key, val):
        if val <= 0:
            return
        if self.seen[e].get(key, 0) >= val:
            return
        self.eng[e].wait_ge(self._handle(key), val)
        self.seen[e][key] = val
        self.nwaits += 1

    def op(self, e, fn, reads=(), writes=(), dma=False):
        deps = {}

        def add(tok):
            if tok is None:
                return
            k, v = tok
            if e == "pe" and k == "pe":
                return
            if deps.get(k, 0) < v:
                deps[k] = v

        for k in reads:
            for kk, vv in self.lastw.get(k, {}).items():
                add((kk, vv))
        for k in writes:
            for kk, vv in self.lastw.get(k, {}).items():
                add((kk, vv))
            for kk, vv in self.readers.get(k, {}).items():
                add((kk, vv))
        for k, v in deps.items():
            self._wait(e, k, v)
        if dma:
            i = self.dnext[e]
            self.dnext[e] = (i + 1) % len(self.dslots[e])
            slot = self.dslots[e][i]
            key = (e, i)
            self._wait(e, key, slot[1])
            inst = fn()
            inst.then_inc(slot[0], 16)
            slot[1] += 16
            tok = (key, slot[1])
        else:
            inst = fn()
            inst.then_inc(self.sem[e], 1)
            self.cnt[e] += 1
            tok = (e, self.cnt[e])
        for k in reads:
            r = self.readers.setdefault(k, {})
            if r.get(tok[0], 0) < tok[1]:
                r[tok[0]] = tok[1]
        for k in writes:
            w = self.lastw.setdefault(k, {})
            if w.get(tok[0], 0) < tok[1]:
                w[tok[0]] = tok[1]
            self.readers[k] = {}
        return tok

    def dma(self, q, out, in_, reads=(), writes=(), **kw):
        return self.op(q, lambda: self.eng[q].dma_start(out=out, in_=in_, **kw), reads, writes, dma=True)

    def barrier(self):
        for e in self.eng:
            for e2 in self.eng:
                if e2 != e:
                    self._wait(e, e2, self.cnt[e2])
            for q, slots in self.dslots.items():
                if q == "pool":
                    continue
                for i, s in enumerate(slots):
                    self._wait(e, (q, i), s[1])
        self.lastw = {k: v for k, v in self.lastw.items() if isinstance(k, tuple) and k[0] in ("WB", "WU", "WD2")}
        self.readers = {}

    def finish(self):
        for q, slots in self.dslots.items():
            for i, s in enumerate(slots):
                self._wait("sp", (q, i), s[1])


class Prefetch:
    def __init__(self, loaders):
        self.loaders = loaders
        self.done = 0

    def ensure(self, i):
        while self.done <= i and self.done < len(self.loaders):
            self.loaders[self.done]()
            self.done += 1


class Scope:
    G = 0

    def __init__(self, cx):
        self.cx = cx
        self.es = contextlib.ExitStack()
        self.n = 0

    def __enter__(self):
        self.es.__enter__()
        return self

    def __exit__(self, *a):
        self.cx.barrier()
        return self.es.__exit__(*a)

    def sb(self, shape, dt, name=None):
        Scope.G += 1
        return self.es.enter_context(self.cx.nc.sbuf_tensor("%s_%d" % (name or "t", Scope.G), list(shape), dt))


def build_program(nseq=2, nlayers=DEPTH, debug=None, stages=ALL_STAGES):
    debug = debug or set()
    nc = bass.Bass("TRN2", target_bir_lowering=False)
    cx = Ctx(nc)

    def dram(name, shape, dt, kind=None):
        if kind is None:
            kind = "ExternalOutput" if name in debug else "Internal"
        return nc.dram_tensor(name, list(shape), dt, kind=kind).ap()

    xin = dram("xin", [nseq, L, D], F32, "ExternalInput")
    hbuf = dram("hbuf", [nseq, L, D], F32, "ExternalOutput")
    W = {}
    wshapes = {
        "w_in": [DEPTH, D, INW], "w_br_ret": [DEPTH, 1024, D], "w_br_diff": [DEPTH, 1024, D],
        "w_br_na": [DEPTH, 1024, D], "w_out": [DEPTH, D, D], "w_up": [DEPTH, D, 2 * DFF],
        "w_down": [DEPTH, DFF, D],
    }
    for k, s in wshapes.items():
        W[k] = dram(k, s, F32, "ExternalInput")
    small = {
        "norm_mix": [DEPTH, D], "norm_ffn": [DEPTH, D], "ret_log2_decay_f": [DEPTH, 8], "ret_log2_decay_b": [DEPTH, 8],
        "ret_out_gain": [DEPTH, 1024], "diff_q_gain": [DEPTH, 128], "diff_k_gain": [DEPTH, 128],
        "diff_lambda": [DEPTH, 4, 128], "diff_out_gain": [DEPTH, 256], "na_q_gain": [DEPTH, 64],
        "na_k_gain": [DEPTH, 64], "ffn_conv_w": [DEPTH, 3, DFF], "ffn_conv_b": [DEPTH, DFF],
    }
    P = {k: dram(k, s, F32, "ExternalInput") for k, s in small.items()}
    c_ident = dram("c_ident", [128, 128], F32, "ExternalInput")
    c_rope_r = dram("c_rope_r", [L, 2, 32], F32, "ExternalInput")
    c_rope_d = dram("c_rope_d", [L, 2, 64], F32, "ExternalInput")

    WB = {k: dram(k + "_bf", s, BF16) for k, s in wshapes.items()}
    QKT = dram("QKT", [nseq, 40, 128, L], BF16)
    VTM = dram("VTM", [nseq, L, 3072], BF16)
    GT = dram("GT", [nseq, 56, 128, L], BF16)
    YT = dram("YT", [nseq, 24, 128, L], BF16)
    c_ret = dram("c_ret", [128, RET_CW], F32, "ExternalInput")
    MT = dram("MT", [nseq, 16, 128, L], BF16)
    na_bt = dram("na_bt", [DEPTH, 16, 128, 960], F32, "ExternalInput")
    c_cm = dram("c_cm", [128, 64], F32, "ExternalInput")

    ps = [cx.es.enter_context(nc.psum_tensor("ps%d" % i, [128, 512], F32)) for i in range(6)]
    psb = [cx.es.enter_context(nc.psum_tensor("psb%d" % i, [128, 1024], BF16)) for i in range(2)]

    cast_q = []

    def emit_cast(l, fn):
        if l == 0:
            fn()
        else:
            cast_q.append(fn)

    def pump(n=12):
        for _ in range(n):
            if cast_q:
                cast_q.pop(0)()

    def cast_weights(l):
        for k, s in wshapes.items():
            if k in ("w_up", "w_down"):
                continue
            rows, cols = s[1], s[2]
            cw = 1920 if cols % 1920 == 0 else (1376 if cols % 1376 == 0 else 2048)
            assert cols % cw == 0
            rg = 256 if rows % 256 == 0 else 128
            for c0 in range(0, cols, cw):
                for r0 in range(0, rows, rg):
                    key = ("WB", k, l, c0 // cw) if k == "w_in" else ("WB", k, l)
                    emit_cast(l, (lambda k=k, l=l, r0=r0, rg=rg, c0=c0, cw=cw, key=key:
                                  cx.dma("pool", WB[k][l, r0:r0 + rg, c0:c0 + cw], W[k][l, r0:r0 + rg, c0:c0 + cw], writes=[key])))

    WD2 = dram("WD2", [DEPTH, 8, 128, 43, 256], BF16)
    WU = dram("WU", [DEPTH, 43, 128, 16, 256], BF16)

    def cast_wdown(l):
        for nb in range(8):
            for fg in range(0, 43, 8):
                f1 = min(fg + 8, 43)
                emit_cast(l, (lambda l=l, nb=nb, fg=fg, f1=f1:
                              cx.dma("pool", WD2[l, nb, :, fg:f1, :],
                                     W["w_down"][l, fg * 128:f1 * 128, nb * 256:(nb + 1) * 256].rearrange("(f p) j -> p f j", p=128),
                                     writes=[("WD2", l)])))

    def cast_wup(l):
        for kc in range(16):
            for half in range(2):
                for fg in range(0, 43, 8):
                    f1 = min(fg + 8, 43)
                    emit_cast(l, (lambda l=l, fg=fg, f1=f1, kc=kc, half=half:
                                  cx.dma("pool", WU[l, fg:f1, :, kc, half * 128:(half + 1) * 128].rearrange("f p j -> p f j"),
                                         W["w_up"][l, kc * 128:(kc + 1) * 128, half * DFF + fg * 128:half * DFF + f1 * 128].rearrange("p (f j) -> p f j", j=128),
                                         writes=[("WU", l)])))

    for l in range(nlayers):
        cast_weights(l)
        cast_wup(l)
        cast_wdown(l)

    cs = Scope(cx)
    cs.es.__enter__()
    ident_f = cs.sb([128, 128], F32, "ident_f")
    ident = cs.sb([128, 128], BF16, "ident")
    cx.dma("sp", ident_f[:], c_ident[:, :], writes=["ident_f"])
    cx.op("dve", lambda: nc.vector.tensor_copy(out=ident[:], in_=ident_f[:]), reads=["ident_f"], writes=["ident"])
    ones_f = cs.sb([128, 128], F32, "ones_f")
    cx.op("dve", lambda: nc.vector.memset(ones_f[:], 1.0), writes=["ones_f"])
    ones_b = cs.sb([128, 128], BF16, "ones_b")
    cx.op("dve", lambda: nc.vector.memset(ones_b[:], 1.0), writes=["ones_b"])
    ones_m = cs.sb([128, 128], BF16, "ones_m")
    cx.op("dve", lambda: nc.vector.memset(ones_m[:], 0.0), writes=["ones_m"])
    cx.op("dve", lambda: nc.vector.memset(ones_m[0:16, :], 1.0), writes=["ones_m"])

    rr = [0]

    def psum_rot(n=6):
        i = rr[0] % n
        rr[0] += 1
        return i

    def norm_to_uT(sc, src, s, gain_ap, uT):
        g_bc = sc.sb([128, D], F32)
        cx.dma("sp", g_bc[:], gain_ap.partition_broadcast(128), writes=[("g_bc", id(g_bc))])
        hts = [sc.sb([128, D], F32) for _ in range(2)]
        sq = sc.sb([128, D], BF16)
        ubs = [sc.sb([128, D], BF16) for _ in range(2)]
        st = [sc.sb([128, 4], F32) for _ in range(2)]
        for ti, (p0, n) in enumerate(TILES):
            b = ti % 2
            ht, ub, stt = hts[b], ubs[b], st[b]
            cx.dma("sp", ht[:n, :], src[s, p0:p0 + n, :], writes=[("ht", b)])
            cx.op("act", lambda: nc.scalar.activation(out=sq[:n, :], in_=ht[:n, :], func=AF.Square,
                                                      accum_out=stt[:n, 0:1]),
                  reads=[("ht", b)], writes=["sq", ("st", b)])
            cx.op("act", lambda: nc.scalar.activation(out=stt[:n, 1:2], in_=stt[:n, 0:1], func=AF.Sqrt,
                                                      scale=1.0 / D, bias=eps_t[:n, 0:1]),
                  reads=[("st", b)], writes=[("st", b)])
            cx.op("dve", lambda: nc.vector.reciprocal(out=stt[:n, 2:3], in_=stt[:n, 1:2]),
                  reads=[("st", b)], writes=[("st", b)])
            cx.op("dve", lambda: nc.vector.scalar_tensor_tensor(out=ub[:n, :], in0=ht[:n, :], scalar=stt[:n, 2:3],
                                                                op0=ALU.mult, in1=g_bc[:n, :], op1=ALU.mult),
                  reads=[("ht", b), ("st", b), ("g_bc", id(g_bc))], writes=[("ub", b)])
            for g4 in range(4):
                pb = psum_rot(2)
                for j in range(4):
                    c = g4 * 4 + j
                    cx.op("pe", lambda: nc.tensor.transpose(out=psb[pb][:, j * 128:j * 128 + n],
                                                            in_=ub[:n, c * 128:(c + 1) * 128], identity=ident[:n, :n]),
                          reads=[("ub", b), "ident"], writes=[("psb", pb)])
                src_v = psb[pb][:, 0:512].rearrange("p (j t) -> p j t", j=4)[:, :, 0:n]
                cx.op("act" if g4 % 2 else "dve",
                      (lambda: nc.scalar.copy(out=uT[:, g4 * 4:g4 * 4 + 4, p0:p0 + n], in_=src_v)) if g4 % 2 else
                      (lambda: nc.vector.tensor_copy(out=uT[:, g4 * 4:g4 * 4 + 4, p0:p0 + n], in_=src_v)),
                      reads=[("psb", pb)], writes=[("uT", ti)])

    UT_KEYS = [("uT", ti) for ti in range(len(TILES))]

    def load_w_block(wt, wsrc, c0, ncols, key, kchunks=16):
        cx.dma("sp", wt[:, :kchunks, :ncols],
               wsrc[:, c0:c0 + ncols].rearrange("(k p) c -> p k c", p=128),
               reads=[key[0]], writes=[key[1]])

    def stage_proj(l, s, src):
        pe2 = "dve"
        E2 = nc.vector if pe2 == "dve" else nc.gpsimd
        with Scope(cx) as sc:
            uT = sc.sb([128, 16, L], BF16, "uT")
            wts = [sc.sb([128, 16, 512], BF16) for _ in range(2)]
            wsrc = WB["w_in"][l]
            all_c0 = ([C_RQ, C_RK, C_DQ, C_DQ + 512, C_DK, C_DK + 512, C_NQ, C_NQ + 512, C_NK, C_NK + 512] +
                      [C_RV, C_RV + 512, C_DV, C_DV + 512, C_NV, C_NV + 512] +
                      [C_RG, C_RG + 512] + [C_GATES + 512 * i for i in range(12)])
            def load_win(i, c0):
                rk = sorted(set([("WB", "w_in", l, c0 // 1920), ("WB", "w_in", l, (c0 + 511) // 1920)]))
                cx.dma("sp", wts[i % 2][:, :, :], wsrc[:, c0:c0 + 512].rearrange("(k p) c -> p k c", p=128),
                       reads=rk, writes=[("wt", i % 2)])

            pf = Prefetch([(lambda i=i, c0=c0: load_win(i, c0)) for i, c0 in enumerate(all_c0)])
            pf.ensure(0)
            with Scope(cx) as scn:
                norm_to_uT(scn, src, s, P["norm_mix"][l], uT)
            wi = [0]

            def next_w(c0):
                i = wi[0]
                wi[0] += 1
                assert all_c0[i] == c0
                pf.ensure(i + 1)
                return wts[i % 2], ("wt", i % 2)

            gq_d = sc.sb([128, 128], F32); gk_d = sc.sb([128, 128], F32)
            gq_n = sc.sb([128, 64], F32); gk_n = sc.sb([128, 64], F32)
            cx.dma("sp", gq_d[:], P["diff_q_gain"][l].partition_broadcast(128), writes=["gq_d"])
            cx.dma("sp", gk_d[:], P["diff_k_gain"][l].partition_broadcast(128), writes=["gk_d"])
            cx.dma("sp", gq_n[:], P["na_q_gain"][l].partition_broadcast(128), writes=["gq_n"])
            cx.dma("sp", gk_n[:], P["na_k_gain"][l].partition_broadcast(128), writes=["gk_n"])
            rope_r = sc.sb([128, 17, 64], F32); rope_d = sc.sb([128, 17, 128], F32)
            for ti, (p0, n) in enumerate(TILES):
                cx.dma("sp", rope_r[:n, ti, :], c_rope_r[p0:p0 + n].rearrange("p a b -> p (a b)"), writes=["rope_r"])
                cx.dma("sp", rope_d[:n, ti, :], c_rope_d[p0:p0 + n].rearrange("p a b -> p (a b)"), writes=["rope_d"])

            qk_blocks = [("rq", C_RQ), ("rk", C_RK), ("dq", C_DQ), ("dq", C_DQ + 512), ("dk", C_DK), ("dk", C_DK + 512),
                         ("nq", C_NQ), ("nq", C_NQ + 512), ("nk", C_NK), ("nk", C_NK + 512)]
            NE = 4
            xs_t = [sc.sb([128, 512], F32) for _ in range(NE)]
            sq_ts = [sc.sb([128, 512], F32) for _ in range(NE)]
            ss_t = [sc.sb([128, 24], F32) for _ in range(NE)]
            t_as = [[sc.sb([128, 256], F32) for _ in range(4)] for _ in range(NE)]
            xo_t = [sc.sb([128, 512], BF16) for _ in range(NE)]
            TT = [sc.sb([128, 4, L], BF16) for _ in range(2)]
            ei = [0]
            tq = []
            for bi, (kind, c0) in enumerate(qk_blocks):
                wt, wkey = next_w(c0)
                Tb = bi % 2
                T = TT[Tb]
                for ti, (p0, n) in enumerate(TILES):
                    pi = psum_rot()
                    for kc in range(16):
                        cx.op("pe", lambda: nc.tensor.matmul(ps[pi][:n, :], lhsT=uT[:, kc, p0:p0 + n], rhs=wt[:, kc, :],
                                                             start=(kc == 0), stop=(kc == 15)),
                              reads=[("uT", ti), wkey], writes=[("ps", pi)])
                    e = ei[0] % NE
                    ei[0] += 1
                    xs, ss, xo = xs_t[e], ss_t[e], xo_t[e]
                    sq_t = sq_ts[e]
                    t_a = t_as[e]
                    kx, ks, ko = ("xs", e), ("ss", e), ("xo", e)
                    ksq = ("sq_t", e)
                    kta = [("ta", e, i) for i in range(4)]
                    sc_k = 0.125 if kind == "rk" else 1.0
                    cx.op("act", lambda: nc.scalar.activation(out=xs[:n, :], in_=ps[pi][:n, :], func=AF.Copy, scale=sc_k),
                          reads=[("ps", pi)], writes=[kx])
                    if kind in ("dq", "dk", "nq", "nk"):
                        dd = 128 if kind[0] == "d" else 64
                        U = 512 // dd
                        gb = {"dq": gq_d, "dk": gk_d, "nq": gq_n, "nk": gk_n}[kind]
                        gkey = {"dq": "gq_d", "dk": "gk_d", "nq": "gq_n", "nk": "gk_n"}[kind]
                        cx.op("act", lambda: nc.scalar.activation(out=sq_t[:n, :], in_=ps[pi][:n, :], func=AF.Square),
                              reads=[("ps", pi)], writes=[ksq])
                        cx.op("dve", lambda: nc.vector.tensor_reduce(out=ss[:n, 0:U], in_=sq_t[:n, :].rearrange("p (u d) -> p u d", u=U),
                                                                     op=ALU.add, axis=AX.X),
                              reads=[ksq], writes=[ks])
                        cx.op("act", lambda: nc.scalar.activation(out=ss[:n, 8:8 + U], in_=ss[:n, 0:U], func=AF.Sqrt,
                                                                  scale=1.0 / dd, bias=eps_t[:n, 0:1]),
                              reads=[ks], writes=[ks])
                        cx.op("dve", lambda: nc.vector.reciprocal(out=ss[:n, 16:16 + U], in_=ss[:n, 8:8 + U]),
                              reads=[ks], writes=[ks])
                        xv = xs[:n, :].rearrange("p (u d) -> p u d", u=U)
                        cx.op("dve", lambda: nc.vector.tensor_tensor(out=xv, in0=xv, in1=ss[:n, 16:16 + U].unsqueeze(2).to_broadcast([n, U, dd]),
                                                                     op=ALU.mult),
                              reads=[kx, ks], writes=[kx])
                        dst = xv if kind[0] == "d" else xo[:n, :].rearrange("p (u d) -> p u d", u=U)
                        cx.op(pe2 if kind[0] == "d" else "dve",
                              (lambda: E2.tensor_tensor(out=dst, in0=xv, in1=gb[:n, :].unsqueeze(1).to_broadcast([n, U, dd]), op=ALU.mult))
                              if kind[0] == "d" else
                              (lambda: nc.vector.tensor_tensor(out=dst, in0=xv, in1=gb[:n, :].unsqueeze(1).to_broadcast([n, U, dd]), op=ALU.mult)),
                              reads=[kx, gkey], writes=[kx] if kind[0] == "d" else [ko])
                    if kind in ("rq", "rk", "dq", "dk"):
                        dd = 64 if kind[0] == "r" else 128
                        hd = dd // 2
                        U = 512 // dd
                        rt = rope_r if kind[0] == "r" else rope_d
                        rkey = "rope_r" if kind[0] == "r" else "rope_d"
                        xv = xs[:n, :].rearrange("p (u a d) -> p u a d", u=U, a=2)
                        ov = xo[:n, :].rearrange("p (u a d) -> p u a d", u=U, a=2)
                        x1, x2 = xv[:, :, 0, :], xv[:, :, 1, :]
                        cosb = rt[:n, ti, 0:hd].unsqueeze(1).to_broadcast([n, U, hd])
                        sinb = rt[:n, ti, hd:2 * hd].unsqueeze(1).to_broadcast([n, U, hd])
                        tv = [t[:n, :].rearrange("p (u d) -> p u d", u=U) for t in t_a]
                        cx.op("dve", lambda: nc.vector.tensor_tensor(out=tv[0], in0=x1, in1=cosb, op=ALU.mult), reads=[kx, rkey], writes=[kta[0]])
                        cx.op(pe2, lambda: E2.tensor_tensor(out=tv[1], in0=x2, in1=sinb, op=ALU.mult), reads=[kx, rkey], writes=[kta[1]])
                        cx.op("dve", lambda: nc.vector.tensor_tensor(out=tv[2], in0=x1, in1=sinb, op=ALU.mult), reads=[kx, rkey], writes=[kta[2]])
                        cx.op(pe2, lambda: E2.tensor_tensor(out=tv[3], in0=x2, in1=cosb, op=ALU.mult), reads=[kx, rkey], writes=[kta[3]])
                        cx.op("dve", lambda: nc.vector.tensor_tensor(out=ov[:, :, 0, :], in0=tv[0], in1=tv[1], op=ALU.subtract),
                              reads=[kta[0], kta[1]], writes=[ko])
                        cx.op(pe2, lambda: E2.tensor_tensor(out=ov[:, :, 1, :], in0=tv[2], in1=tv[3], op=ALU.add),
                              reads=[kta[2], kta[3]], writes=[ko])
                    def do_transposes(xo=xo, ko=ko, n=n, p0=p0, T=T, Tb=Tb):
                        pb = psum_rot(2)
                        for j in range(4):
                            cx.op("pe", lambda: nc.tensor.transpose(out=psb[pb][:, j * 128:j * 128 + n], in_=xo[:n, j * 128:(j + 1) * 128],
                                                                    identity=ident[:n, :n]),
                                  reads=[ko, "ident"], writes=[("psb", pb)])
                        src_v = psb[pb][:, 0:512].rearrange("p (j t) -> p j t", j=4)[:, :, 0:n]
                        cx.op("dve", lambda: nc.vector.tensor_copy(out=T[:, :, p0:p0 + n], in_=src_v),
                              reads=[("psb", pb)], writes=[("T", Tb)])
                    tq.append(do_transposes)
                    if len(tq) > 2:
                        tq.pop(0)()
                while tq:
                    tq.pop(0)()
                cx.dma("sp", QKT[s, bi * 4:(bi + 1) * 4].rearrange("j p t -> p j t"), T[:, :, :],
                       reads=[("T", Tb)], writes=[("QKT", s, bi)])

            v_blocks = [C_RV, C_RV + 512, C_DV, C_DV + 512, C_NV, C_NV + 512]
            vo_t = [sc.sb([128, 512], BF16) for _ in range(2)]
            for bi, c0 in enumerate(v_blocks):
                wt, wkey = next_w(c0)
                for ti, (p0, n) in enumerate(TILES):
                    pi = psum_rot()
                    for kc in range(16):
                        cx.op("pe", lambda: nc.tensor.matmul(ps[pi][:n, :], lhsT=uT[:, kc, p0:p0 + n], rhs=wt[:, kc, :],
                                                             start=(kc == 0), stop=(kc == 15)),
                              reads=[("uT", ti), wkey], writes=[("ps", pi)])
                    e = ei[0] % 2
                    ei[0] += 1
                    vo = vo_t[e]
                    cx.op("act" if e else "dve",
                          (lambda: nc.scalar.copy(out=vo[:n, :], in_=ps[pi][:n, :])) if e else
                          (lambda: nc.vector.tensor_copy(out=vo[:n, :], in_=ps[pi][:n, :])),
                          reads=[("ps", pi)], writes=[("vo", e)])
                    cx.dma("sp", VTM[s, p0:p0 + n, bi * 512:(bi + 1) * 512], vo[:n, :], reads=[("vo", e)], writes=[("VTM", s)])

            f_blocks = [(C_RG + 512 * i, AF.Silu) for i in range(2)] + [(C_GATES + 512 * i, AF.Sigmoid) for i in range(12)]
            go_t = [sc.sb([128, L], BF16) for _ in range(2)]
            gi = [0]
            for bi, (c0, fn) in enumerate(f_blocks):
                wt, wkey = next_w(c0)
                for j in range(4):
                    g = gi[0] % 2
                    gi[0] += 1
                    go = go_t[g]
                    for (q0, qn) in BLOCKS:
                        pi = psum_rot()
                        for kc in range(16):
                            cx.op("pe", lambda: nc.tensor.matmul(ps[pi][:, :qn], lhsT=wt[:, kc, j * 128:(j + 1) * 128], rhs=uT[:, kc, q0:q0 + qn],
                                                                 start=(kc == 0), stop=(kc == 15)),
                                  reads=UT_KEYS + [wkey], writes=[("ps", pi)])
                        cx.op("act", lambda: nc.scalar.activation(out=go[:, q0:q0 + qn], in_=ps[pi][:, :qn], func=fn),
                              reads=[("ps", pi)], writes=[("go", g)])
                    cx.dma("sp", GT[s, bi * 4 + j], go[:, :], reads=[("go", g)], writes=[("GT", s)])

    def stage_ret(l, s):
        with Scope(cx) as sc, nc.allow_non_contiguous_dma(reason="tiny param loads"):
            cr = sc.sb([128, RET_CW], F32, "c_ret_sb")
            cx.dma("sp", cr[:], c_ret[:, :], writes=["cr"])
            XF = cr[:, 0:2048]; XBr = cr[:, 2048:4096]
            X0f = cr[:, 4096:4224]; X0b = cr[:, 4224:4352]; Mge = cr[:, 4352:4480]; Mlt = cr[:, 4480:4608]
            XMK = cr[:, 4608:6656]; XMQ = cr[:, 6656:6912]
            dec = sc.sb([128, 16], F32); tt_ = sc.sb([128, 16], F32); pp = sc.sb([128, 16], F32); LG = sc.sb([128, 16], F32)
            cx.dma("sp", dec[:, 0:8], P["ret_log2_decay_f"][l].partition_broadcast(128), writes=["dec"])
            cx.dma("sp", dec[:, 8:16], P["ret_log2_decay_b"][l].partition_broadcast(128), writes=["dec"])
            cx.op("act", lambda: nc.scalar.activation(out=tt_[:], in_=dec[:], func=AF.Exp, scale=-math.log(2.0)), reads=["dec"], writes=["tt_"])
            cx.op("dve", lambda: nc.vector.tensor_scalar(out=pp[:], in0=tt_[:], scalar1=1.0 / 6.0, scalar2=0.2, op0=ALU.mult, op1=ALU.add),
                  reads=["tt_"], writes=["pp"])
            for cst in (0.25, 1.0 / 3.0, 0.5, 1.0):
                cx.op("dve", lambda: nc.vector.tensor_tensor(out=pp[:], in0=pp[:], in1=tt_[:], op=ALU.mult), reads=["pp", "tt_"], writes=["pp"])
                cx.op("dve", lambda: nc.vector.tensor_scalar(out=pp[:], in0=pp[:], scalar1=cst, scalar2=None, op0=ALU.add),
                      reads=["pp"], writes=["pp"])
            cx.op("dve", lambda: nc.vector.scalar_tensor_tensor(out=LG[:], in0=pp[:], scalar=-1.0, op0=ALU.mult, in1=tt_[:], op1=ALU.mult),
                  reads=["pp", "tt_"], writes=["LG"])
            gain = sc.sb([128, 8], F32)
            cx.dma("sp", gain[:], P["ret_out_gain"][l].rearrange("(h p) -> p h", p=128), writes=["gain"])
            TTs = [sc.sb([128, 33, 128], F32) for _ in range(2)]
            TMKs = [sc.sb([128, 16, 128], F32) for _ in range(2)]
            TMMs = [sc.sb([128, 16], F32) for _ in range(2)]
            TMQs = [sc.sb([128, 16, 16], F32) for _ in range(2)]
            E1 = sc.sb([128, 128], F32); E2 = sc.sb([128, 128], F32)
            KTs = [sc.sb([128, L], BF16) for _ in range(2)]
            QTs = [sc.sb([128, L], BF16) for _ in range(2)]
            Vs = [sc.sb([128, 17, 128], BF16) for _ in range(2)]
            for b in range(2):
                cx.op("pool", lambda: nc.gpsimd.memset(KTs[b][:], 0.0), writes=[("KT", b)])
                cx.op("pool", lambda: nc.gpsimd.memset(QTs[b][:], 0.0), writes=[("QT", b)])
                cx.op("pool", lambda: nc.gpsimd.memset(Vs[b][:, 0, :], 0.0), writes=[("V", b)])
                cx.op("pool", lambda: nc.gpsimd.memset(TMKs[b][:].rearrange("p a b -> p (a b)"), 0.0), writes=[("TT", b)])
                cx.op("pool", lambda: nc.gpsimd.memset(TMMs[b][:], 0.0), writes=[("TT", b)])
            SGs = [sc.sb([128, L], BF16) for _ in range(2)]
            YTs = [sc.sb([128, L], BF16) for _ in range(2)]
            Pts = [sc.sb([128, 512], BF16) for _ in range(4)]
            sqs = sc.sb([128, 512], F32); rss = sc.sb([128, 512], F32); t1s = sc.sb([128, 512], F32)
            Scp = [sc.sb([128, 512], F32) for _ in range(2)]
            sci = [0]
            pti = [0]

            def load_head(h):
                hb = h % 2
                TT, TMK, TMQ, KT, QT, V, SG = TTs[hb], TMKs[hb], TMQs[hb], KTs[hb], QTs[hb], Vs[hb], SGs[hb]
                TMM = TMMs[hb]
                kTT = ("TT", hb)
                lgf = LG[:, h:h + 1]; lgb = LG[:, 8 + h:9 + h]
                cx.dma("sp", KT[0:64, :], QKT[s, 4 + h // 2, (h % 2) * 64:(h % 2) * 64 + 64, :], writes=[("KT", hb)])
                cx.dma("sp", QT[0:64, :], QKT[s, h // 2, (h % 2) * 64:(h % 2) * 64 + 64, :], writes=[("QT", hb)])
                cx.dma("sp", V[:16, 0, :], VTM[s, 0:16, h * 128:(h + 1) * 128], writes=[("V", hb)])
                cx.dma("sp", V[:, 1:17, :], VTM[s, 16:L, h * 128:(h + 1) * 128].rearrange("(t p) c -> p t c", p=128), writes=[("V", hb)])
                cx.dma("sp", SG[:, :], GT[s, h], writes=[("SG", hb)])
                cx.op("act", lambda: nc.scalar.activation(out=TT[:, 17:33, :].rearrange("p a b -> p (a b)"), in_=XF, func=AF.Exp, scale=lgf),
                      reads=["cr", "LG"], writes=[kTT])
                cx.op("act", lambda: nc.scalar.activation(out=TT[:, 0:16, :].rearrange("p a b -> p (a b)"), in_=XBr, func=AF.Exp, scale=lgb),
                      reads=["cr", "LG"], writes=[kTT])
                cx.op("act", lambda: nc.scalar.activation(out=E1[:], in_=X0f, func=AF.Exp, scale=lgf), reads=["cr", "LG"], writes=["E1"])
                cx.op("act", lambda: nc.scalar.activation(out=E2[:], in_=X0b, func=AF.Exp, scale=lgb), reads=["cr", "LG"], writes=["E2"])
                cx.op("pool", lambda: nc.gpsimd.tensor_tensor(out=E1[:], in0=E1[:], in1=Mge, op=ALU.mult), reads=["E1", "cr"], writes=["E1"])
                cx.op("pool", lambda: nc.gpsimd.tensor_tensor(out=E2[:], in0=E2[:], in1=Mlt, op=ALU.mult), reads=["E2", "cr"], writes=["E2"])
                cx.op("pool", lambda: nc.gpsimd.tensor_tensor(out=TT[:, 16, :], in0=E1[:], in1=E2[:], op=ALU.add), reads=["E1", "E2"], writes=[kTT])
                cx.op("pool", lambda: nc.gpsimd.tensor_tensor(out=TMM[0:16, :], in0=E1[0:16, 0:16], in1=E2[0:16, 0:16], op=ALU.add), reads=["E1", "E2"], writes=[kTT])
                cx.op("act", lambda: nc.scalar.activation(out=TMK[0:16, :, :].rearrange("p a b -> p (a b)"), in_=XMK[0:16, :], func=AF.Exp, scale=lgf[0:16, :]),
                      reads=["cr", "LG"], writes=[kTT])
                cx.op("act", lambda: nc.scalar.activation(out=TMQ[:, :, :].rearrange("p a b -> p (a b)"), in_=XMQ, func=AF.Exp, scale=lgb),
                      reads=["cr", "LG"], writes=[kTT])

            pf = Prefetch([(lambda h=h: load_head(h)) for h in range(8)])
            pf.ensure(0)
            for h in range(8):
                hb = h % 2
                pf.ensure(h + 1)
                pump()
                TT, TMK, TMQ, KT, QT, V, SG, YTt = TTs[hb], TMKs[hb], TMQs[hb], KTs[hb], QTs[hb], Vs[hb], SGs[hb], YTs[hb]
                TMM = TMMs[hb]
                kTT = ("TT", hb)
                RT = [(0, 128)] + TILES[1:]
                for qi, (q0, N) in enumerate(BLOCKS):
                    po = 4 + (qi % 2)
                    pend = []

                    def emit_av(kt, M, pt):
                        cx.op("pe", lambda: nc.tensor.matmul(ps[po][:, :N], lhsT=V[:M, kt, :], rhs=Pts[pt][:M, :N], start=(kt == 0), stop=(kt == 16)),
                              reads=[("V", hb), ("Pt", pt)], writes=[("ps", po)])

                    for kt, (k0, M) in enumerate(RT):
                        pi = psum_rot(4)
                        cx.op("pe", lambda: nc.tensor.matmul(ps[pi][:M, :N], lhsT=KT[:, k0:k0 + M], rhs=QT[:, q0:q0 + N], start=True, stop=True),
                              reads=[("KT", hb), ("QT", hb)], writes=[("ps", pi)])
                        if kt == 0 and qi == 0:
                            mult = TMM[:, :]
                        elif kt == 0:
                            i0 = 4 * (qi - 1)
                            mult = TMK[:, i0:i0 + 4, :].rearrange("p a b -> p (a b)")
                        elif qi == 0:
                            mult = TMQ[:, kt - 1, :]
                        else:
                            i0 = 4 * (qi - 1); j = kt - 1
                            mult = TT[:, i0 - j + 16:i0 - j + 20, :].rearrange("p a b -> p (a b)")
                        pt = pti[0] % 4
                        pti[0] += 1
                        Pt = Pts[pt]
                        if False:
                            sb_ = sci[0] % 2
                            sci[0] += 1
                            cx.op("act", lambda: nc.scalar.copy(out=Scp[sb_][:M, :N], in_=ps[pi][:M, :N]), reads=[("ps", pi)], writes=[("Scp", sb_)])
                            cx.op("pool", lambda: nc.gpsimd.tensor_tensor(out=Pt[:M, :N], in0=Scp[sb_][:M, :N], in1=mult, op=ALU.mult),
                                  reads=[("Scp", sb_), kTT], writes=[("Pt", pt)])
                        else:
                            cx.op("dve", lambda: nc.vector.tensor_tensor(out=Pt[:M, :N], in0=ps[pi][:M, :N], in1=mult, op=ALU.mult),
                                  reads=[("ps", pi), kTT], writes=[("Pt", pt)])
                        pend.append((kt, M, pt))
                        if len(pend) > 2:
                            emit_av(*pend.pop(0))
                    while pend:
                        emit_av(*pend.pop(0))
                    cx.op("act", lambda: nc.scalar.activation(out=sqs[:, :N], in_=ps[po][:, :N], func=AF.Square), reads=[("ps", po)], writes=["sqs"])
                    pi = psum_rot(4)
                    cx.op("pe", lambda: nc.tensor.matmul(ps[pi][:, :N], lhsT=ones_f[:, :], rhs=sqs[:, :N], start=True, stop=True),
                          reads=["ones_f", "sqs"], writes=[("ps", pi)])
                    cx.op("act", lambda: nc.scalar.activation(out=rss[:, :N], in_=ps[pi][:, :N], func=AF.Ln, scale=1.0 / 128.0, bias=eps_t[:, 0:1]),
                          reads=[("ps", pi), "eps_t"], writes=["rss"])
                    cx.op("act", lambda: nc.scalar.activation(out=rss[:, :N], in_=rss[:, :N], func=AF.Exp, scale=-0.5), reads=["rss"], writes=["rss"])
                    cx.op("dve", lambda: nc.vector.tensor_tensor(out=t1s[:, :N], in0=ps[po][:, :N], in1=rss[:, :N], op=ALU.mult),
                          reads=[("ps", po), "rss"], writes=["t1s"])
                    cx.op("dve", lambda: nc.vector.scalar_tensor_tensor(out=YTt[:, q0:q0 + N], in0=t1s[:, :N], scalar=gain[:, h:h + 1], op0=ALU.mult,
                                                                        in1=SG[:, q0:q0 + N], op1=ALU.mult),
                          reads=["t1s", "gain", ("SG", hb)], writes=[("YTt", hb)])
                cx.dma("sp", YT[s, h], YTt[:, :], reads=[("YTt", hb)], writes=[("YT", s)])

    def stage_diff(l, s):
        lambda_init = 0.8 - 0.6 * math.exp(-0.3 * l)
        with Scope(cx) as sc, nc.allow_non_contiguous_dma(reason="tiny param loads"):
            lv = sc.sb([128, 4, 128], F32); lp = sc.sb([128, 2, 128], F32); ls = sc.sb([128, 4], F32); nlam = sc.sb([128, 1], F32)
            cx.dma("sp", lv[:].rearrange("p a b -> p (a b)"), P["diff_lambda"][l].rearrange("a b -> (a b)").partition_broadcast(128), writes=["lv"])
            cx.op("dve", lambda: nc.vector.tensor_tensor(out=lp[:, 0, :], in0=lv[:, 0, :], in1=lv[:, 1, :], op=ALU.mult), reads=["lv"], writes=["lp"])
            cx.op("dve", lambda: nc.vector.tensor_tensor(out=lp[:, 1, :], in0=lv[:, 2, :], in1=lv[:, 3, :], op=ALU.mult), reads=["lv"], writes=["lp"])
            cx.op("dve", lambda: nc.vector.tensor_reduce(out=ls[:, 0:2], in_=lp[:], op=ALU.add, axis=AX.X), reads=["lp"], writes=["ls"])
            cx.op("act", lambda: nc.scalar.activation(out=ls[:, 2:4], in_=ls[:, 0:2], func=AF.Exp), reads=["ls"], writes=["ls"])
            cx.op("dve", lambda: nc.vector.tensor_tensor(out=nlam[:], in0=ls[:, 3:4], in1=ls[:, 2:3], op=ALU.subtract), reads=["ls"], writes=["nlam"])
            cx.op("dve", lambda: nc.vector.tensor_scalar(out=nlam[:], in0=nlam[:], scalar1=-lambda_init, scalar2=None, op0=ALU.add),
                  reads=["nlam"], writes=["nlam"])
            gain = sc.sb([128, 2], F32)
            cx.dma("sp", gain[:], P["diff_out_gain"][l].rearrange("(c p) -> p c", p=128), writes=["gain"])
            cx.op("dve", lambda: nc.vector.tensor_scalar(out=gain[:], in0=gain[:], scalar1=1.0 - lambda_init, scalar2=None, op0=ALU.mult),
                  reads=["gain"], writes=["gain"])
            KTs = [sc.sb([128, 2, L], BF16) for _ in range(2)]
            QTs = [sc.sb([128, 2, L], BF16) for _ in range(2)]
            Vs = [sc.sb([128, 17, 256], BF16) for _ in range(2)]
            for b in range(2):
                cx.op("pool", lambda: nc.gpsimd.memset(Vs[b][:, 0, :], 0.0), writes=[("V", b)])
            YTs = [sc.sb([128, 2, L], BF16) for _ in range(2)]
            RT = [(0, 128)] + TILES[1:]
            Pts = [sc.sb([128, 512], BF16) for _ in range(4)]
            ON = sc.sb([128, 2, 2, 512], F32)
            rcp = sc.sb([128, 512], F32); osb = sc.sb([128, 2, 512], F32); sqs = sc.sb([128, 2, 512], F32); rss = sc.sb([128, 512], F32)
            pti = [0]
            scale = 128.0 ** -0.5
            import os
            CUT = int(os.environ.get("DIFF_CUT", "9"))
            def load_head(h):
                hb = h % 2
                KT, QT, V = KTs[hb], QTs[hb], Vs[hb]
                cx.dma("sp", KT[:, :, :], QKT[s, 16 + 2 * h:18 + 2 * h].rearrange("c p t -> p c t"), writes=[("KT", hb)])
                cx.dma("sp", QT[:, :, :], QKT[s, 8 + 2 * h:10 + 2 * h].rearrange("c p t -> p c t"), writes=[("QT", hb)])
                cx.dma("sp", V[:16, 0, :], VTM[s, 0:16, 1024 + h * 256:1024 + (h + 1) * 256], writes=[("V", hb)])
                for tg in range(2):
                    cx.dma("sp", V[:, 1 + 8 * tg:9 + 8 * tg, :],
                           VTM[s, 16 + 1024 * tg:16 + 1024 * (tg + 1), 1024 + h * 256:1024 + (h + 1) * 256].rearrange("(t p) c -> p t c", p=128),
                           writes=[("V", hb)])

            pf = Prefetch([(lambda h=h: load_head(h)) for h in range(4)])
            pf.ensure(0)
            for h in range(4 if CUT > 0 else 0):
                hb = h % 2
                pf.ensure(h + 1)
                pump()
                KT, QT, V, YTt = KTs[hb], QTs[hb], Vs[hb], YTs[hb]
                for qi, (q0, N) in enumerate(BLOCKS if CUT > 1 else []):
                    for half in range(2):
                        pend = []

                        def emit_av(kt, M, pt):
                            Pt = Pts[pt]
                            for c in range(2):
                                cx.op("pe", lambda: nc.tensor.matmul(ps[3 + c][:, :N], lhsT=V[:M, kt, c * 128:(c + 1) * 128], rhs=Pt[:M, :N],
                                                                     start=(kt == 0), stop=(kt == 16)),
                                      reads=[("V", hb), ("Pt", pt)], writes=[("ps", 3 + c)])
                            cx.op("pe", lambda: nc.tensor.matmul(ps[5][:, :N], lhsT=(ones_m if kt == 0 else ones_b)[:M, :], rhs=Pt[:M, :N], start=(kt == 0), stop=(kt == 16)),
                                  reads=["ones_b", "ones_m", ("Pt", pt)], writes=[("ps", 5)])

                        for kt, (k0, M) in enumerate(RT):
                            pi = psum_rot(3)
                            cx.op("pe", lambda: nc.tensor.matmul(ps[pi][:M, :N], lhsT=KT[:, half, k0:k0 + M], rhs=QT[:, half, q0:q0 + N], start=True, stop=True),
                                  reads=[("KT", hb), ("QT", hb)], writes=[("ps", pi)])
                            pt = pti[0] % 4
                            pti[0] += 1
                            Pt = Pts[pt]
                            cx.op("act", lambda: nc.scalar.activation(out=Pt[:M, :N], in_=ps[pi][:M, :N], func=AF.Exp, scale=scale),
                                  reads=[("ps", pi)], writes=[("Pt", pt)])
                            pend.append((kt, M, pt))
                            if len(pend) > 2:
                                emit_av(*pend.pop(0))
                        while pend:
                            emit_av(*pend.pop(0))
                        if CUT <= 2:
                            continue
                        cx.op("dve", lambda: nc.vector.reciprocal(out=rcp[:, :N], in_=ps[5][:, :N]), reads=[("ps", 5)], writes=["rcp"])
                        for c in range(2):
                            cx.op("dve", lambda: nc.vector.tensor_tensor(out=ON[:, half, c, :N], in0=ps[3 + c][:, :N], in1=rcp[:, :N], op=ALU.mult),
                                  reads=[("ps", 3 + c), "rcp"], writes=[("ON", half)])
                    if CUT <= 3:
                        continue
                    for c in range(2):
                        cx.op("dve", lambda: nc.vector.scalar_tensor_tensor(out=osb[:, c, :N], in0=ON[:, 1, c, :N], scalar=nlam[:, 0:1], op0=ALU.mult,
                                                                            in1=ON[:, 0, c, :N], op1=ALU.add),
                              reads=[("ON", 0), ("ON", 1), "nlam"], writes=["osb"])
                        cx.op("pool", lambda: nc.gpsimd.tensor_tensor(out=sqs[:, c, :N], in0=osb[:, c, :N], in1=osb[:, c, :N], op=ALU.mult),
                              reads=["osb"], writes=["sqs"])
                    if CUT <= 5:
                        continue
                    pi = psum_rot(3)
                    for c in range(2):
                        cx.op("pe", lambda: nc.tensor.matmul(ps[pi][:, :N], lhsT=ones_f[:, :], rhs=sqs[:, c, :N], start=(c == 0), stop=(c == 1)),
                              reads=["ones_f", "sqs"], writes=[("ps", pi)])
                    cx.op("act", lambda: nc.scalar.activation(out=rss[:, :N], in_=ps[pi][:, :N], func=AF.Sqrt, scale=1.0 / 256.0, bias=eps_t[:, 0:1]),
                          reads=[("ps", pi), "eps_t"], writes=["rss"])
                    cx.op("dve", lambda: nc.vector.reciprocal(out=rss[:, :N], in_=rss[:, :N]), reads=["rss"], writes=["rss"])
                    if CUT <= 6:
                        continue
                    for c in range(2):
                        cx.op("dve", lambda: nc.vector.tensor_tensor(out=sqs[:, c, :N], in0=osb[:, c, :N], in1=rss[:, :N], op=ALU.mult),
                              reads=["osb", "rss"], writes=["sqs"])
                        cx.op("dve", lambda: nc.vector.tensor_scalar(out=YTt[:, c, q0:q0 + N], in0=sqs[:, c, :N], scalar1=gain[:, c:c + 1], scalar2=None,
                                                                     op0=ALU.mult),
                              reads=["sqs", "gain"], writes=[("YTt", hb)])
                cx.dma("sp", YT[s, 8 + 2 * h:10 + 2 * h].rearrange("c p t -> p c t"), YTt[:, :, :], reads=[("YTt", hb)], writes=[("YT", s)])

    def stage_na(l, s):
        with Scope(cx) as sc:
            cm = sc.sb([128, 64], F32, "cm")
            cx.dma("sp", cm[:], c_cm[:, :], writes=["cm"])
            bts = [[sc.sb([128, 15, 64], F32) for _ in range(2)] for _ in range(2)]
            EBf = [[sc.sb([128, 16, 64], BF16) for _ in range(2)] for _ in range(2)]
            EBi = [[sc.sb([128, 24, 64], BF16) for _ in range(2)] for _ in range(2)]
            KTs = [[sc.sb([128, L], BF16) for _ in range(2)] for _ in range(2)]
            QTs = [sc.sb([128, L], BF16) for _ in range(2)]
            Vz = [[sc.sb([128, 17, 128], BF16) for _ in range(2)] for _ in range(2)]
            for x in range(2):
                for b in range(2):
                    cx.op("pool", lambda: nc.gpsimd.memset(EBi[x][b][:], 0.0), writes=[("EB", x, b)])
                    cx.op("pool", lambda: nc.gpsimd.memset(EBf[x][b][:], 0.0), writes=[("EB", x, b)])
                    cx.op("pool", lambda: nc.gpsimd.memset(Vz[x][b][:], 0.0), writes=[("Vz", x, b)])
                    cx.op("pool", lambda: nc.gpsimd.memset(KTs[x][b][:], 0.0), writes=[("KT", x, b)])
            oh = [sc.sb([128, 128], BF16) for _ in range(2)]
            for b in range(2):
                cx.op("pool", lambda: nc.gpsimd.memset(oh[b][:], 0.0), writes=["oh"])
                cx.op("pool", lambda: nc.gpsimd.memset(oh[b][:, b * 64:(b + 1) * 64], 1.0), writes=["oh"])
            ohm = [sc.sb([128, 128], BF16) for _ in range(2)]
            for b in range(2):
                cx.op("pool", lambda: nc.gpsimd.memset(ohm[b][:], 0.0), writes=["oh"])
                cx.op("pool", lambda: nc.gpsimd.memset(ohm[b][0:16, b * 64:(b + 1) * 64], 1.0), writes=["oh"])
            YTs = [sc.sb([128, L], BF16) for _ in range(2)]
            Ets = [sc.sb([128, 512], BF16) for _ in range(5)]
            Pts = [sc.sb([128, 512], BF16) for _ in range(5)]
            rcp = sc.sb([128, 512], F32)
            pti = [0]
            sna = [0]
            blkc = [0]
            QB = [(0, 4, list(range(0, 4)), "f")] + [(4 + 8 * g, 8, list(range(4 * g, 4 * g + 8)), "i") for g in range(3)] + [(28, 4, list(range(12, 16)), "f")]

            def load_pair(hp):
                x = hp % 2
                cx.dma("sp", QTs[x][:, :], QKT[s, 24 + hp], writes=[("QT", x)])
                for par in range(2):
                    h = 2 * hp + par
                    KT, V = KTs[x][par], Vz[x][par]
                    bt, ebf, ebi = bts[x][par], EBf[x][par], EBi[x][par]
                    kEB = ("EB", x, par)
                    cx.dma("sp", bt[:].rearrange("p a b -> p (a b)"), na_bt[l, h], writes=[("bt", x, par)])
                    cx.dma("sp", KT[par * 64:par * 64 + 64, :], QKT[s, 32 + hp, par * 64:par * 64 + 64, :], writes=[("KT", x, par)])
                    cx.dma("sp", V[:16, 0, par * 64:par * 64 + 64], VTM[s, 0:16, 2048 + h * 64:2048 + (h + 1) * 64], writes=[("Vz", x, par)])
                    for tg in range(4):
                        cx.dma("sp", V[:, 1 + 4 * tg:5 + 4 * tg, par * 64:par * 64 + 64],
                               VTM[s, 16 + 512 * tg:16 + 512 * (tg + 1), 2048 + h * 64:2048 + (h + 1) * 64].rearrange("(t p) c -> p t c", p=128),
                               writes=[("Vz", x, par)])
                    cx.op("act", lambda: nc.scalar.activation(out=bt[:], in_=bt[:], func=AF.Exp), reads=[("bt", x, par)], writes=[("bt", x, par)])
                    cx.op("dve", lambda: nc.vector.tensor_tensor(out=ebf[0:64, 0:15, :], in0=bt[0:64], in1=cm[0:64, :].unsqueeze(1).to_broadcast([64, 15, 64]), op=ALU.mult),
                          reads=[("bt", x, par), "cm"], writes=[kEB])
                    cx.op("pool", lambda: nc.gpsimd.tensor_tensor(out=ebf[64:128, 1:16, :], in0=bt[64:128], in1=cm[64:128, :].unsqueeze(1).to_broadcast([64, 15, 64]), op=ALU.mult),
                          reads=[("bt", x, par), "cm"], writes=[kEB])
                    cx.op("pool", lambda: nc.gpsimd.tensor_copy(out=ebi[0:64, 8:16, :], in_=ebf[0:64, 4:12, :]), reads=[kEB], writes=[kEB])
                    cx.op("pool", lambda: nc.gpsimd.tensor_copy(out=ebi[64:128, 9:17, :], in_=ebf[64:128, 5:13, :]), reads=[kEB], writes=[kEB])

            pf = Prefetch([(lambda hp=hp: load_pair(hp)) for hp in range(8)])
            pf.ensure(0)
            for hp in range(8):
                x = hp % 2
                pf.ensure(hp + 1)
                pump()
                YTt = YTs[x]
                blocks = [(0, 16, None, None, None)] + [(16 + 64 * r0, 64 * R, kts, kind, r0) for (r0, R, kts, kind) in QB]
                for (q0, N, kts, kind, r0) in blocks:
                    pa, pb = (2, 3) if blkc[0] % 2 == 0 else (4, 5)
                    blkc[0] += 1
                    items = []
                    for par in range(2):
                        klist = [(-1, 0, 128)] + ([(j, 16 + 128 * j, 128) for j in kts] if kts is not None else [])
                        for (j, k0, M) in klist:
                            items.append((par, j, k0, M))
                    pend = []

                    def emit_av(idx, par, j, M, pt):
                        V = Vz[x][par]
                        Pt = Pts[pt]
                        first = (idx == 0); last = (idx == len(items) - 1)
                        cx.op("pe", lambda: nc.tensor.matmul(ps[pa][:, :N], lhsT=V[:M, j + 1, :], rhs=Pt[:M, :N], start=first, stop=last),
                              reads=[("Vz", x, par), ("Pt", pt)], writes=[("ps", pa)])
                        cx.op("pe", lambda: nc.tensor.matmul(ps[pb][:, :N], lhsT=(ohm if j < 0 else oh)[par][:M, :], rhs=Pt[:M, :N], start=first, stop=last),
                              reads=["oh", ("Pt", pt)], writes=[("ps", pb)])

                    for idx, (par, j, k0, M) in enumerate(items):
                        KT, QT = KTs[x][par], QTs[x]
                        ebf, ebi = EBf[x][par], EBi[x][par]
                        kEB = ("EB", x, par)
                        pi = sna[0] % 2
                        sna[0] += 1
                        cx.op("pe", lambda: nc.tensor.matmul(ps[pi][:M, :N], lhsT=KT[:, k0:k0 + M], rhs=QT[:, q0:q0 + N], start=True, stop=True),
                              reads=[("KT", x, par), ("QT", x)], writes=[("ps", pi)])
                        pt = pti[0] % 5
                        pti[0] += 1
                        Et, Pt = Ets[pt], Pts[pt]
                        if j < 0:
                            cx.op("act", lambda: nc.scalar.activation(out=Pt[:M, :N], in_=ps[pi][:M, :N], func=AF.Exp, scale=0.125),
                                  reads=[("ps", pi)], writes=[("Pt", pt)])
                        else:
                            cx.op("act", lambda: nc.scalar.activation(out=Et[:M, :N], in_=ps[pi][:M, :N], func=AF.Exp, scale=0.125),
                                  reads=[("ps", pi)], writes=[("Et", pt)])
                            R = N // 64
                            e0 = 7 - 2 * j + r0
                            if kind == "f":
                                assert 1 <= e0 and e0 + R <= 15, (e0, R)
                                tab = ebf[:, e0:e0 + R, :]
                            else:
                                i0 = e0 + 4
                                assert 1 <= i0 and i0 + R <= 23, (i0, R)
                                tab = ebi[:, i0:i0 + R, :]
                            tab = tab.rearrange("p a b -> p (a b)")
                            cx.op("dve", lambda: nc.vector.tensor_tensor(out=Pt[:, :N], in0=Et[:, :N], in1=tab, op=ALU.mult),
                                  reads=[("Et", pt), kEB], writes=[("Pt", pt)])
                        pend.append((idx, par, j, M, pt))
                        if len(pend) > 3:
                            emit_av(*pend.pop(0))
                    while pend:
                        emit_av(*pend.pop(0))
                    cx.op("dve", lambda: nc.vector.reciprocal(out=rcp[:, :N], in_=ps[pb][:, :N]), reads=[("ps", pb)], writes=["rcp"])
                    cx.op("dve", lambda: nc.vector.tensor_tensor(out=YTt[:, q0:q0 + N], in0=ps[pa][:, :N], in1=rcp[:, :N], op=ALU.mult),
                          reads=[("ps", pa), "rcp"], writes=[("YTt", x)])
                cx.dma("sp", YT[s, 16 + hp], YTt[:, :], reads=[("YTt", x)], writes=[("YT", s)])

    def stage_merge(l, s, src):
        with Scope(cx) as sc:
            Y = sc.sb([128, 24, L], BF16, "Yall")
            for c3 in range(6):
                cx.dma("sp", Y[:, 4 * c3:4 * c3 + 4, :], YT[s, 4 * c3:4 * c3 + 4].rearrange("c p t -> p c t"), writes=["Y"])
            wbs = [sc.sb([128, 24, 512], BF16) for _ in range(2)]
            gts = [sc.sb([128, 3, L], BF16) for _ in range(2)]
            mts = [sc.sb([128, L], BF16) for _ in range(2)]
            macc = [sc.sb([128, 512], F32) for _ in range(2)]
            mtmp = [sc.sb([128, 512], F32) for _ in range(2)]
            wnames = ["w_br_ret", "w_br_diff", "w_br_na"]
            oi = [0]

            def load_wb(og):
                for b in range(3):
                    cx.dma("sp", wbs[og % 2][:, 8 * b:8 * b + 8, :], WB[wnames[b]][l][:, og * 512:(og + 1) * 512].rearrange("(k p) c -> p k c", p=128),
                           reads=[("WB", wnames[b], l)], writes=[("wb", og % 2)])

            def load_gt(oc):
                for b in range(3):
                    cx.dma("sp", gts[oc % 2][:, b, :], GT[s, 8 + 16 * b + oc], writes=[("gt", oc % 2)])

            pf_wb = Prefetch([(lambda og=og: load_wb(og)) for og in range(4)])
            pf_gt = Prefetch([(lambda oc=oc: load_gt(oc)) for oc in range(16)])
            pf_wb.ensure(0)
            pf_gt.ensure(0)
            for og in range(4):
                wb = wbs[og % 2]
                pf_wb.ensure(og + 1)
                pump()
                for j in range(4):
                    oc = og * 4 + j
                    o = oi[0] % 2
                    oi[0] += 1
                    gt, mt = gts[o], mts[o]
                    pf_gt.ensure(oc + 1)
                    for bi, (q0, N) in enumerate(BLOCKS):
                        m = bi % 2
                        for b in range(3):
                            pi = psum_rot()
                            for kc in range(8):
                                cx.op("pe", lambda: nc.tensor.matmul(ps[pi][:, :N], lhsT=wb[:, 8 * b + kc, j * 128:(j + 1) * 128], rhs=Y[:, 8 * b + kc, q0:q0 + N],
                                                                     start=(kc == 0), stop=(kc == 7)),
                                      reads=[("wb", og % 2), "Y"], writes=[("ps", pi)])
                            dst = macc[m] if b == 0 else mtmp[m]
                            dk = ("macc", m) if b == 0 else ("mtmp", m)
                            cx.op("dve", lambda: nc.vector.tensor_tensor(out=dst[:, :N], in0=ps[pi][:, :N], in1=gt[:, b, q0:q0 + N], op=ALU.mult),
                                  reads=[("ps", pi), ("gt", o)], writes=[dk])
                            if b == 1:
                                cx.op("pool", lambda: nc.gpsimd.tensor_tensor(out=macc[m][:, :N], in0=macc[m][:, :N], in1=mtmp[m][:, :N], op=ALU.add),
                                      reads=[("macc", m), ("mtmp", m)], writes=[("macc", m)])
                            if b == 2:
                                cx.op("pool", lambda: nc.gpsimd.tensor_tensor(out=mt[:, q0:q0 + N], in0=macc[m][:, :N], in1=mtmp[m][:, :N], op=ALU.add),
                                      reads=[("macc", m), ("mtmp", m)], writes=[("mt", o)])
                    cx.dma("sp", MT[s, oc], mt[:, :], reads=[("mt", o)], writes=[("MT", s)])
        with Scope(cx) as sc:
            M_ = sc.sb([128, 16, L], BF16, "Mall")
            for c4 in range(4):
                cx.dma("sp", M_[:, 4 * c4:4 * c4 + 4, :], MT[s, 4 * c4:4 * c4 + 4].rearrange("c p t -> p c t"), writes=["M_"])
            wos = [sc.sb([128, 16, 512], BF16) for _ in range(2)]
            hts = [sc.sb([128, 512], F32) for _ in range(3)]
            hi = [0]
            pf_wo = Prefetch([(lambda cb=cb: load_w_block(wos[cb % 2], WB["w_out"][l], cb * 512, 512, (("WB", "w_out", l), ("wo", cb % 2))))
                              for cb in range(4)])
            pf_wo.ensure(0)
            for cb in range(4):
                wo = wos[cb % 2]
                pf_wo.ensure(cb + 1)
                for ti, (p0, n) in enumerate(TILES):
                    pi = psum_rot()
                    for kc in range(16):
                        cx.op("pe", lambda: nc.tensor.matmul(ps[pi][:n, :], lhsT=M_[:, kc, p0:p0 + n], rhs=wo[:, kc, :], start=(kc == 0), stop=(kc == 15)),
                              reads=["M_", ("wo", cb % 2)], writes=[("ps", pi)])
                    hh = hi[0] % 3
                    hi[0] += 1
                    ht = hts[hh]
                    cx.dma("sp", ht[:n, :], src[s, p0:p0 + n, cb * 512:(cb + 1) * 512], reads=[("hsrc", s, ti, cb)], writes=[("ht", hh)])
                    cx.op("dve", lambda: nc.vector.tensor_tensor(out=ht[:n, :], in0=ps[pi][:n, :], in1=ht[:n, :], op=ALU.add),
                          reads=[("ps", pi), ("ht", hh)], writes=[("ht", hh)])
                    cx.dma("sp", hbuf[s, p0:p0 + n, cb * 512:(cb + 1) * 512], ht[:n, :], reads=[("ht", hh)], writes=[("hsrc", s, ti, cb)])

    FBLOCKS = [(510 * b, min(510, L - 510 * b)) for b in range(5)]

    def stage_ffn(l, s):
        with Scope(cx) as sc:
            uT = sc.sb([128, 16, L], BF16, "uT2")
            with Scope(cx) as sc2:
                norm_to_uT(sc2, hbuf, s, P["norm_ffn"][l], uT)
            cwT = sc.sb([43, 4, 128], F32); cwb = sc.sb([128, 4, 64], F32)
            cx.dma("sp", cwT[:, 0:3, :], P["ffn_conv_w"][l].rearrange("j (f p) -> f j p", p=128), writes=["cwT"])
            cx.dma("sp", cwT[:, 3, :], P["ffn_conv_b"][l].rearrange("(f p) -> f p", p=128), writes=["cwT"])
            pi = psum_rot()
            for j in range(4):
                cx.op("pe", lambda: nc.tensor.transpose(out=ps[pi][:, j * 64:j * 64 + 43], in_=cwT[:, j, :], identity=ident_f[:43, :43]),
                      reads=["cwT", "ident_f"], writes=[("ps", pi)])
            cx.op("dve", lambda: nc.vector.tensor_copy(out=cwb[:].rearrange("p a b -> p (a b)"), in_=ps[pi][:, 0:256]), reads=[("ps", pi)], writes=["cwb"])
            HA = sc.sb([128, 43, 512], BF16, "HA")
            wus = [sc.sb([128, 16, 256], BF16) for _ in range(2)]
            wds = [sc.sb([128, 43, 256], BF16) for _ in range(2)]
            gs = [sc.sb([128, 514], F32) for _ in range(2)]
            acc = [sc.sb([128, 512], F32) for _ in range(2)]
            sgt = [sc.sb([128, 512], F32) for _ in range(2)]
            hts = [sc.sb([128, 256], F32) for _ in range(3)]
            for b in range(2):
                cx.op("pool", lambda: nc.gpsimd.memset(gs[b][:], 0.0), writes=[("gs", b)])
            wi = [0]; di = [0]; hi = [0]

            def load_wd(i):
                nb = i % 8
                cx.dma("sp", wds[i % 2][:, :, :], WD2[l, nb], reads=[("WD2", l)], writes=[("wd", i % 2)])

            pf_wu = Prefetch([(lambda i=i: cx.dma("sp", wus[i % 2][:, :, :], WU[l, i % 43], reads=[("WU", l)], writes=[("wu", i % 2)]))
                              for i in range(43 * len(FBLOCKS))])
            pf_wd = Prefetch([(lambda i=i: load_wd(i)) for i in range(8 * len(FBLOCKS))])
            pf_wu.ensure(0)
            pf_wd.ensure(0)
            for bi, (c0, n) in enumerate(FBLOCKS):
                pump(16)
                w0 = max(c0 - 1, 0); w1 = min(c0 + n + 1, L); NW = w1 - w0
                goff = w0 - (c0 - 1)
                for f in range(43):
                    wb_ = wi[0] % 2
                    pf_wu.ensure(wi[0] + 1)
                    wi[0] += 1
                    wu = wus[wb_]
                    pg = psum_rot(); pv = psum_rot()
                    for half, pp_ in ((0, pg), (1, pv)):
                        for kc in range(16):
                            cx.op("pe", lambda: nc.tensor.matmul(ps[pp_][:, :NW], lhsT=wu[:, kc, half * 128:(half + 1) * 128], rhs=uT[:, kc, w0:w1],
                                                                 start=(kc == 0), stop=(kc == 15)),
                                  reads=UT_KEYS + [("wu", wb_)], writes=[("ps", pp_)])
                    g = f % 2
                    G, A, SGt = gs[g], acc[g], sgt[g]
                    if goff > 0 or bi == len(FBLOCKS) - 1:
                        cx.op("pool", lambda: nc.gpsimd.memset(G[:], 0.0), writes=[("gs", g)])
                    cx.op("act", lambda: nc.scalar.copy(out=G[:, goff:goff + NW], in_=ps[pg][:, :NW]), reads=[("ps", pg)], writes=[("gs", g)])
                    cx.op("dve", lambda: nc.vector.tensor_scalar(out=A[:, :n], in0=G[:, 0:n], scalar1=cwb[:, 0, f:f + 1], scalar2=None, op0=ALU.mult),
                          reads=[("gs", g), "cwb"], writes=[("acc", g)])
                    for j in (1, 2):
                        cx.op("dve", lambda: nc.vector.scalar_tensor_tensor(out=A[:, :n], in0=G[:, j:j + n], scalar=cwb[:, j, f:f + 1], op0=ALU.mult,
                                                                            in1=A[:, :n], op1=ALU.add),
                              reads=[("gs", g), "cwb", ("acc", g)], writes=[("acc", g)])
                    cx.op("act", lambda: nc.scalar.activation(out=SGt[:, :n], in_=A[:, :n], func=AF.Silu, bias=cwb[:, 3, f:f + 1]),
                          reads=[("acc", g), "cwb"], writes=[("sgt", g)])
                    voff = c0 - w0
                    cx.op("dve", lambda: nc.vector.tensor_tensor(out=HA[:, f, :n], in0=ps[pv][:, voff:voff + n], in1=SGt[:, :n], op=ALU.mult),
                          reads=[("ps", pv), ("sgt", g)], writes=[("HA", f)])
                HA_KEYS = [("HA", f) for f in range(43)]
                for nb in range(8):
                    db = di[0] % 2
                    pf_wd.ensure(di[0] + 1)
                    di[0] += 1
                    wd = wds[db]
                    for t0 in range(0, n, 128):
                        tn = min(128, n - t0)
                        pi = psum_rot()
                        for f in range(43):
                            cx.op("pe", lambda: nc.tensor.matmul(ps[pi][:tn, :256], lhsT=HA[:, f, t0:t0 + tn], rhs=wd[:, f, :], start=(f == 0), stop=(f == 42)),
                                  reads=HA_KEYS + [("wd", db)], writes=[("ps", pi)])
                        hh = hi[0] % 3
                        hi[0] += 1
                        ht = hts[hh]
                        r0 = c0 + t0
                        cx.dma("sp", ht[:tn, :], hbuf[s, r0:r0 + tn, nb * 256:(nb + 1) * 256], reads=[("hb", s, bi)], writes=[("ht", hh)])
                        cx.op("dve", lambda: nc.vector.tensor_tensor(out=ht[:tn, :], in0=ps[pi][:tn, :256], in1=ht[:tn, :], op=ALU.add),
                              reads=[("ps", pi), ("ht", hh)], writes=[("ht", hh)])
                        cx.dma("sp", hbuf[s, r0:r0 + tn, nb * 256:(nb + 1) * 256], ht[:tn, :], reads=[("ht", hh)], writes=[("hb2", s, bi)])

    eps_t = cs.sb([128, 1], F32, "eps_t")
    cx.op("dve", lambda: nc.vector.memset(eps_t[:], EPS), writes=["eps_t"])
    cx.barrier()

    for l in range(nlayers):
        src = xin if l == 0 else hbuf
        if l >= 1:
            pump(len(cast_q))
        for s in range(nseq):
            if "proj" in stages:
                stage_proj(l, s, src)
            if "ret" in stages:
                stage_ret(l, s)
            if "diff" in stages:
                stage_diff(l, s)
            if "na" in stages:
                stage_na(l, s)
            if "merge" in stages:
                stage_merge(l, s, src)
            if "ffn" in stages:
                stage_ffn(l, s)

    if "merge" not in stages:
        for s in range(nseq):
            cx.dma("sp", hbuf[s], xin[s], writes=[("hbuf", s)])

    cx.barrier()
    cx.finish()
    cs.es.__exit__(None, None, None)
    cx.es.close()
    return nc


def host_consts():
    c = {}
    c["c_ident"] = np.eye(128, dtype=np.float32)
    pos = np.arange(L, dtype=np.float32)
    for name, d in (("c_rope_r", 64), ("c_rope_d", 128)):
        inv = np.power(np.float32(10000.0), -np.arange(0, d, 2, dtype=np.float32) / np.float32(d)).astype(np.float32)
        ang = pos[:, None] * inv[None, :]
        c[name] = np.stack([np.cos(ang), np.sin(ang)], axis=1).astype(np.float32)
    a = np.arange(128, dtype=np.float32)[None, :]
    b = np.arange(128, dtype=np.float32)[:, None]
    cr = np.zeros((128, RET_CW), np.float32)
    XF = np.stack([128.0 * d + a - b for d in range(1, 17)], axis=1)
    XBr = np.stack([128.0 * (16 - dd) + b - a for dd in range(16)], axis=1)
    cr[:, 0:2048] = XF.reshape(128, 2048)
    cr[:, 2048:4096] = XBr.reshape(128, 2048)
    cr[:, 4096:4224] = a - b
    cr[:, 4224:4352] = b - a
    cr[:, 4352:4480] = (a >= b)
    cr[:, 4480:4608] = (a < b)
    XMK = np.stack([16.0 + 128.0 * i + a - b for i in range(16)], axis=1)
    cr[:, 4608:6656] = XMK.reshape(128, 2048)
    a16 = np.arange(16, dtype=np.float32)[None, :]
    XMQ = np.stack([16.0 + 128.0 * j + b - a16 for j in range(16)], axis=1)
    cr[:, 6656:6912] = XMQ.reshape(128, 256)
    c["c_ret"] = cr
    kc = (np.arange(128) % 64)[:, None]
    qc = np.arange(64)[None, :]
    cstart = np.clip(qc - 8, 0, 48)
    c["c_cm"] = ((kc >= cstart) & (kc < cstart + 16)).astype(np.float32)
    return c


def na_bias_table(rpb):
    kc = (np.arange(128) % 64)[:, None]
    qc = np.arange(64)[None, :]
    dc = np.clip(kc - qc + 15, 0, 30)
    e = np.arange(15)
    bt = rpb[:, :, (14 - e)[None, :, None], dc[:, None, :]]
    return np.ascontiguousarray(bt.reshape(rpb.shape[0], 16, 128, 960).astype(np.float32))


WNAMES = ["w_in", "w_br_ret", "w_br_diff", "w_br_na", "w_out", "w_up", "w_down"]
SNAMES = ["norm_mix", "norm_ffn", "ret_log2_decay_f", "ret_log2_decay_b", "ret_out_gain", "diff_q_gain", "diff_k_gain",
          "diff_lambda", "diff_out_gain", "na_q_gain", "na_k_gain", "ffn_conv_w", "ffn_conv_b"]


def make_in_maps(inputs, nseq, ncores):
    x = np.asarray(inputs["x"], dtype=np.float32)
    meta = np.asarray(inputs["meta_tokens"], dtype=np.float32)
    consts = host_consts()
    shared = {k: np.ascontiguousarray(np.asarray(inputs[k], dtype=np.float32)) for k in WNAMES + SNAMES}
    shared.update(consts)
    shared["na_bt"] = na_bias_table(np.asarray(inputs["na_rpb"], dtype=np.float32))
    maps = []
    for c in range(ncores):
        xin = np.empty((nseq, L, D), np.float32)
        for s in range(nseq):
            xin[s, :NMETA] = meta
            xin[s, NMETA:] = x[c * nseq + s]
        m = dict(shared)
        m["xin"] = xin
        maps.append(m)
    return maps


def kernel(**inputs):
    nseq = 2
    nc = build_program(nseq=nseq)
    maps = make_in_maps(inputs, nseq, NCORES)
    res = run_bass_kernel_spmd(nc, maps, core_ids=list(range(NCORES)))
    out = np.empty((16, SEQ, D), np.float32)
    for c in range(NCORES):
        hb = res.results[c]["hbuf"]
        for s in range(nseq):
            out[c * nseq + s] = hb[s, NMETA:]
    return out
```
